# Optimizing a Trainium2 kernel written in Bass

```python
import math
import jax
import jax.numpy as jnp
from jax import lax
import numpy as np

D_MODEL = 1024
BATCH = 8
SEQ = 4096
DEPTH = 1

HEAD_DIM = 64
A_Q_HEADS = 8
A_KV_HEADS = 2
A_WINDOW = 128
B_GROUPS = ((128, 1), (512, 4), (2048, 16))
B_HEADS_PER_GROUP = 8
A_WIDTH = A_Q_HEADS * HEAD_DIM
B_WIDTH = B_HEADS_PER_GROUP * HEAD_DIM
N_BRANCHES = 2
N_BUCKETS = 32
MAX_DISTANCE = 1024
REL_HEADS = A_Q_HEADS + len(B_GROUPS) * B_HEADS_PER_GROUP
PROJ_SIZES = (A_WIDTH, A_KV_HEADS * HEAD_DIM, A_KV_HEADS * HEAD_DIM,
              len(B_GROUPS) * B_WIDTH, len(B_GROUPS) * B_WIDTH, len(B_GROUPS) * B_WIDTH,
              A_WIDTH, B_WIDTH, N_BRANCHES * D_MODEL)
IN_WIDTH = sum(PROJ_SIZES)
EPS = 1e-6
NEG_INF = -1e30

kernel_name = "hybrid_gated_window_dilated_attention_block"


def rms_norm(x, gain):
    xf = x.astype(jnp.float32)
    y = xf * lax.rsqrt(jnp.mean(xf * xf, axis=-1, keepdims=True) + EPS)
    return (y * gain.astype(jnp.float32)).astype(x.dtype)


def t5_bucket(rel):
    half = N_BUCKETS // 2
    max_exact = half // 2
    ret = (rel > 0).astype(jnp.int32) * half
    n = jnp.abs(rel)
    nf = jnp.maximum(n, max_exact).astype(jnp.float32)
    large = max_exact + (jnp.log(nf / max_exact) / math.log(MAX_DISTANCE / max_exact)
                         * (half - max_exact)).astype(jnp.int32)
    large = jnp.minimum(large, half - 1)
    return ret + jnp.where(n < max_exact, n, large)


def banded_attention(q, k, v, half_window, stride, bias_table, sink):
    bsz, L, H, dh = q.shape
    KV = k.shape[2]
    G = H // KV
    blk = half_window
    nb = -(-L // blk)
    Lp = nb * blk
    pad = Lp - L
    qb = jnp.pad(q, ((0, 0), (0, pad), (0, 0), (0, 0))).reshape(bsz, nb, blk, KV, G, dh)

    def windows(t):
        tp = jnp.pad(t, ((0, 0), (blk, blk + pad), (0, 0), (0, 0))).reshape(bsz, nb + 2, blk, KV, dh)
        return jnp.concatenate([tp[:, :-2], tp[:, 1:-1], tp[:, 2:]], axis=2)

    kw = windows(k)
    vw = windows(v)
    scores = jnp.einsum('bnqkgd,bnmkd->bkgnqm', qb, kw).astype(jnp.float32)

    qi = jnp.arange(blk)
    mi = jnp.arange(3 * blk)
    rel = mi[None, :] - blk - qi[:, None]
    bias = bias_table[t5_bucket(rel * stride)]
    bias = bias.transpose(2, 0, 1).reshape(KV, G, 1, blk, 3 * blk).astype(jnp.float32)
    blocks = jnp.arange(nb)[:, None, None] * blk
    qpos = blocks + qi[None, :, None]
    kpos = blocks - blk + mi[None, None, :]
    valid = (kpos >= 0) & (kpos < L) & (jnp.abs(kpos - qpos) <= half_window)

    logits = jnp.where(valid, scores + bias, NEG_INF)
    lse = jax.nn.logsumexp(logits, axis=-1)
    if sink is not None:
        lse = jnp.logaddexp(lse, sink.astype(jnp.float32).reshape(KV, G, 1, 1))
    probs = jnp.exp(logits - lse[..., None])
    out = jnp.einsum('bkgnqm,bnmkd->bnqkgd', probs.astype(v.dtype), vw)
    out = out.reshape(bsz, Lp, H, dh)[:, :L]
    lse = lse.transpose(0, 3, 4, 1, 2).reshape(bsz, Lp, H)[:, :L]
    return out, lse


def dilated_group(q, k, v, window, dilation, bias_table):
    bsz, S, H, dh = q.shape
    L = S // dilation

    def to_sub(t):
        return t.reshape(bsz, L, dilation, H, dh).transpose(0, 2, 1, 3, 4).reshape(bsz * dilation, L, H, dh)

    out, lse = banded_attention(to_sub(q), to_sub(k), to_sub(v), window // (2 * dilation),
                                dilation, bias_table, None)
    out = out.reshape(bsz, dilation, L, H, dh).transpose(0, 2, 1, 3, 4).reshape(bsz, S, H, dh)
    lse = lse.reshape(bsz, dilation, L, H).transpose(0, 2, 1, 3).reshape(bsz, S, H)
    return out, lse


def head_rms_norm(t, gain):
    return rms_norm(t, gain)


def setup_inputs(seed: int = 0) -> dict:
    key = jax.random.key(seed)
    ks = jax.random.split(key, 16)
    f32 = jnp.float32
    x = jax.random.normal(ks[0], (BATCH, SEQ, D_MODEL), f32)
    norm_gain = 1.0 + 0.02 * jax.random.normal(ks[1], (DEPTH, D_MODEL), f32)
    w_in = jax.random.normal(ks[2], (DEPTH, D_MODEL, IN_WIDTH), f32) * D_MODEL ** -0.5
    q_norm_a = 1.0 + 0.02 * jax.random.normal(ks[3], (DEPTH, HEAD_DIM), f32)
    k_norm_a = 1.0 + 0.02 * jax.random.normal(ks[4], (DEPTH, HEAD_DIM), f32)
    q_norm_b = 1.0 + 0.02 * jax.random.normal(ks[5], (DEPTH, HEAD_DIM), f32)
    k_norm_b = 1.0 + 0.02 * jax.random.normal(ks[6], (DEPTH, HEAD_DIM), f32)
    sink_a = 0.5 * jax.random.normal(ks[7], (DEPTH, A_Q_HEADS), f32)
    rel_bias = 0.5 * jax.random.normal(ks[8], (N_BUCKETS, REL_HEADS), f32)
    w_branch_a = jax.random.normal(ks[9], (DEPTH, A_WIDTH, D_MODEL), f32) * A_WIDTH ** -0.5
    w_branch_b = jax.random.normal(ks[10], (DEPTH, B_WIDTH, D_MODEL), f32) * B_WIDTH ** -0.5
    b_merge = 0.1 * jax.random.normal(ks[11], (DEPTH, N_BRANCHES, D_MODEL), f32)
    w_out = jax.random.normal(ks[12], (DEPTH, D_MODEL, D_MODEL), f32) * D_MODEL ** -0.5
    return {"x": x, "norm_gain": norm_gain, "w_in": w_in, "q_norm_a": q_norm_a,
            "k_norm_a": k_norm_a, "q_norm_b": q_norm_b, "k_norm_b": k_norm_b,
            "sink_a": sink_a, "rel_bias": rel_bias, "w_branch_a": w_branch_a,
            "w_branch_b": w_branch_b, "b_merge": b_merge, "w_out": w_out}


def reference(x, norm_gain, w_in, q_norm_a, k_norm_a, q_norm_b, k_norm_b, sink_a,
              rel_bias, w_branch_a, w_branch_b, b_merge, w_out):
    bsz, S, D = x.shape
    n_groups = len(B_GROUPS)
    scale = HEAD_DIM ** -0.5
    split_idx = [int(i) for i in np.cumsum(PROJ_SIZES)[:-1]]
    for layer in range(DEPTH):
        h = rms_norm(x, norm_gain[layer])
        proj = h @ w_in[layer]
        qa, ka, va, qb, kb, vb, ga, gb, mg = jnp.split(proj, split_idx, axis=-1)

        qa = head_rms_norm(qa.reshape(bsz, S, A_Q_HEADS, HEAD_DIM), q_norm_a[layer]) * scale
        ka = head_rms_norm(ka.reshape(bsz, S, A_KV_HEADS, HEAD_DIM), k_norm_a[layer])
        va = va.reshape(bsz, S, A_KV_HEADS, HEAD_DIM)
        ya, _ = banded_attention(qa, ka, va, A_WINDOW, 1, rel_bias[:, :A_Q_HEADS], sink_a[layer])
        ya = ya.reshape(bsz, S, A_WIDTH) * jax.nn.silu(ga)

        qb = head_rms_norm(qb.reshape(bsz, S, n_groups, B_HEADS_PER_GROUP, HEAD_DIM), q_norm_b[layer]) * scale
        kb = head_rms_norm(kb.reshape(bsz, S, n_groups, B_HEADS_PER_GROUP, HEAD_DIM), k_norm_b[layer])
        vb = vb.reshape(bsz, S, n_groups, B_HEADS_PER_GROUP, HEAD_DIM)
        outs = []
        lses = []
        for g, (window, dilation) in enumerate(B_GROUPS):
            c0 = A_Q_HEADS + g * B_HEADS_PER_GROUP
            o, l = dilated_group(qb[:, :, g], kb[:, :, g], vb[:, :, g], window, dilation,
                                 rel_bias[:, c0:c0 + B_HEADS_PER_GROUP])
            outs.append(o)
            lses.append(l)
        alpha = jax.nn.softmax(jnp.stack(lses, axis=0), axis=0)
        yb = jnp.sum(alpha[..., None].astype(x.dtype) * jnp.stack(outs, axis=0), axis=0)
        yb = yb.reshape(bsz, S, B_WIDTH) * jax.nn.silu(gb)

        br_a = ya @ w_branch_a[layer]
        br_b = yb @ w_branch_b[layer]
        gates = jax.nn.sigmoid(mg.reshape(bsz, S, N_BRANCHES, D).astype(jnp.float32)
                               + b_merge[layer].astype(jnp.float32)).astype(x.dtype)
        merged = gates[:, :, 0] * br_a + gates[:, :, 1] * br_b
        x = x + merged @ w_out[layer]
    return x
```

```python
import math
import numpy as np
import concourse.bass as bass
import concourse.mybir as mybir
from concourse.bass_utils import run_bass_kernel_spmd

F32 = mybir.dt.float32
BF = mybir.dt.bfloat16
AF = mybir.ActivationFunctionType
ALU = mybir.AluOpType

NT = 4096
DM = 1024
EPS = 1e-6
NEG = -30000.0
MG_OFF = 16 * 512
WCOLS = MG_OFF + 2048

UNITS = []
for j in range(2):
    for i in range(2):
        ci = 2 * j + i
        UNITS.append(dict(kind="A", D=1, W=128, ci=ci, gate=True, ychunk=ci, g=0,
                          q=(4 * j + 2 * i) * 64, k=[512 + 64 * j, 512 + 64 * j],
                          v=[640 + 64 * j, 640 + 64 * j], gt=5376 + ci * 128))
for hp in range(4):
    for g, D in enumerate((1, 4, 16)):
        UNITS.append(dict(kind="B", D=D, W=64, ci=hp, gate=(g == 2), ychunk=4 + hp, g=g,
                          q=768 + g * 512 + hp * 128,
                          k=[2304 + g * 512 + hp * 128, 2304 + g * 512 + hp * 128 + 64],
                          v=[3840 + g * 512 + hp * 128, 3840 + g * 512 + hp * 128 + 64],
                          gt=5888 + hp * 128))


def _unit_cols(u):
    cols = list(range(u["q"], u["q"] + 128))
    cols += list(range(u["k"][0], u["k"][0] + 64)) + list(range(u["k"][1], u["k"][1] + 64))
    cols += list(range(u["v"][0], u["v"][0] + 64)) + list(range(u["v"][1], u["v"][1] + 64))
    cols += list(range(u["gt"], u["gt"] + 128))
    return cols


def _t5_bucket(rel):
    rel = np.asarray(rel, dtype=np.int64)
    half, max_exact = 16, 8
    ret = (rel > 0).astype(np.int64) * half
    n = np.abs(rel)
    nf = np.maximum(n, max_exact).astype(np.float32)
    large = max_exact + (np.log(nf / np.float32(max_exact)) / np.float32(math.log(1024 / max_exact))
                         * np.float32(half - max_exact)).astype(np.int32)
    large = np.minimum(large, half - 1)
    return ret + np.where(n < max_exact, n, large)


class Op:
    __slots__ = ("eng", "fn", "deps", "pos", "inc", "ms", "dma", "stream", "ordn")

    def __init__(self, eng, fn, dma=False, stream=None):
        self.eng, self.fn, self.dma, self.stream = eng, fn, dma, stream
        self.deps = []
        self.inc = False
        self.ms = 0
        self.ordn = 0


class Prog:
    ENG = ("pe", "act", "dve", "pool", "sp")

    def __init__(self, nc, sems, dsems):
        self.nc = nc
        self.sems = sems
        self.dsems = dsems
        self.stream_sem = {}
        self.stream_cnt = {}
        self.stream_last = {}
        self.ms_cnt = {e: 0 for e in self.ENG}
        self.lw = {}
        self.rd = {}
        self.seen = {e: {} for e in self.ENG}
        self.alias = {}
        self.reset()

    def reset(self):
        self.ops = {e: [] for e in self.ENG}

    def _expand(self, toks):
        out = []
        for t in toks:
            out.extend(self.alias.get(t, (t,)))
        return out

    def op(self, eng, fn, reads=(), writes=(), dma=False, stream=None):
        o = Op(eng, fn, dma, stream)
        reads = self._expand(reads)
        writes = self._expand(writes)
        deps = []
        for t in reads:
            w = self.lw.get(t)
            if w is not None:
                deps.append(w)
        for t in writes:
            w = self.lw.get(t)
            if w is not None:
                deps.append(w)
            for r in self.rd.get(t, {}).values():
                deps.append(r)
        if dma:
            if stream not in self.stream_sem:
                self.stream_sem[stream] = self.dsems.pop()
                self.stream_cnt[stream] = 0
            prev = self.stream_last.get(stream)
            if prev is not None:
                deps.append(prev)
            self.stream_cnt[stream] += 1
            o.ordn = self.stream_cnt[stream]
            self.stream_last[stream] = o
        for d in deps:
            if d is o:
                continue
            if (not d.dma) and d.eng == "pe" and eng == "pe" and not dma:
                continue
            o.deps.append(d)
            d.inc = True
        for t in reads:
            key = ("dma", stream) if dma else eng
            self.rd.setdefault(t, {})[key] = o
        for t in writes:
            self.lw[t] = o
            self.rd[t] = {}
        self.ops[eng].append(o)
        return o

    def emit(self, block):
        nc = self.nc
        for e in self.ENG:
            for o in self.ops[e]:
                if o.inc and not o.dma:
                    self.ms_cnt[e] += 1
                    o.ms = self.ms_cnt[e]
        prog = self

        def run(e, engobj):
            seen = prog.seen[e]
            for o in prog.ops[e]:
                need = {}
                for d in o.deps:
                    if d.dma:
                        s = prog.stream_sem[d.stream]
                        v = 16 * d.ordn
                    else:
                        s = prog.sems[d.eng]
                        v = d.ms
                    key = id(s)
                    if need.get(key, (None, 0))[1] < v:
                        need[key] = (s, v)
                for key, (s, v) in need.items():
                    if seen.get(key, 0) < v:
                        engobj.wait_ge(s, v)
                        seen[key] = v
                if o.fn is None:
                    continue
                ins = o.fn(engobj)
                if o.dma:
                    ins.then_inc(prog.stream_sem[o.stream], 16)
                elif o.inc:
                    ins.then_inc(prog.sems[e], 1)

        @block.tensor
        def _(eng):
            run("pe", eng)

        @block.scalar
        def _(eng):
            run("act", eng)

        @block.vector
        def _(eng):
            run("dve", eng)

        @block.gpsimd
        def _(eng):
            run("pool", eng)

        @block.sync
        def _(eng):
            run("sp", eng)

        self.reset()


hT_all = ["hT_%d" % i for i in range(8)]


def AP(t, off, dims):
    return bass.AP(t, off, [list(d) for d in dims])


def build_nc(debug=None):
    nc = bass.Bass("TRN2", target_bir_lowering=False)
    dt_in = lambda n, s: nc.dram_tensor(n, s, F32, kind="ExternalInput")
    x_d = dt_in("x", [NT, DM])
    wcat_d = dt_in("wcat", [DM, WCOLS])
    wa_d = dt_in("w_a", [512, DM])
    wb_d = dt_in("w_b", [512, DM])
    wo_d = dt_in("w_o", [DM, DM])
    gT_d = dt_in("gT", [128, 8])
    nrm_d = dt_in("nrm", [128, 4])
    sink_d = dt_in("sinkT", [128, 4])
    bm_d = dt_in("bmT", [128, 16])
    bA_d = dt_in("biasA", [8, 128, 384])
    bB_d = dt_in("biasB", [24, 128, 256])
    id_d = dt_in("ident", [128, 128])
    bo_d = dt_in("bones", [128, 128])
    y_d = nc.dram_tensor("y", [NT, DM], F32, kind="ExternalOutput")
    yscr_d = nc.dram_tensor("yscr", [8, 128, NT], BF, kind="ExternalOutput" if debug else "Internal")
    if debug:
        hTd = nc.dram_tensor("hT_dbg", [128, 8 * NT], BF, kind="ExternalOutput")
        accNd = nc.dram_tensor("accN_dbg", [128, NT], F32, kind="ExternalOutput")
        accDd = nc.dram_tensor("accD_dbg", [128, NT], F32, kind="ExternalOutput")
        QTd = nc.dram_tensor("QT_dbg", [128, NT], BF, kind="ExternalOutput")
        KTd = nc.dram_tensor("KT_dbg", [128, NT], BF, kind="ExternalOutput")
        VTd = nc.dram_tensor("VT_dbg", [128, NT], BF, kind="ExternalOutput")
        gated = nc.dram_tensor("gate_dbg", [128, NT], BF, kind="ExternalOutput")

    from contextlib import ExitStack
    with ExitStack() as es:
        def sb(name, shape, dt):
            return es.enter_context(nc.sbuf_tensor(name, shape, dt))

        sems = {e: es.enter_context(nc.semaphore("s_" + e)) for e in ("pe", "act", "dve", "pool")}
        dsems = [es.enter_context(nc.semaphore("d%d" % i)) for i in range(40)]
        P = Prog(nc, sems, dsems)
        ps = [es.enter_context(nc.psum_tensor("ps%d" % i, [128, 1024], F32)) for i in range(4)]
        psbf = [p.bitcast(BF) for p in ps]
        for k_ in range(4):
            for h_ in "ab":
                P.alias["ps%d%s" % (k_, h_)] = ("ps%d%s_0" % (k_, h_), "ps%d%s_1" % (k_, h_))

        def bank(k):
            return ps[k // 2], (k % 2) * 512, "ps%d%s" % (k // 2, "ab"[k % 2])

        hT = sb("hT", [128, 8 * NT], BF)
        ident = sb("ident_s", [128, 128], BF)
        bones = sb("bones_s", [128, 128], BF)
        ones64 = sb("ones64", [128, 64], BF)
        gT = sb("gT_s", [128, 8], F32)
        nrm = sb("nrm_s", [128, 4], F32)
        g8 = sb("g8", [128, 2], F32)
        sinkT = sb("sink_s", [128, 4], F32)
        esink = sb("esink", [128, 4], F32)
        bmT = sb("bm_s", [128, 16], F32)
        mhalf = sb("mhalf", [128, 1], F32)

        P.op("pool", lambda e: e.dma_start(out=ident[:, :], in_=id_d.ap()), writes=["ident"], dma=True, stream="c_id")
        P.op("pool", lambda e: e.dma_start(out=bones[:, :], in_=bo_d.ap()), writes=["bones"], dma=True, stream="c_bo")
        P.op("sp", lambda e: e.dma_start(out=gT[:, :], in_=gT_d.ap()), writes=["gT"], dma=True, stream="c_gT")
        P.op("sp", lambda e: e.dma_start(out=nrm[:, :], in_=nrm_d.ap()), writes=["nrm"], dma=True, stream="c_nrm")
        P.op("sp", lambda e: e.dma_start(out=sinkT[:, :], in_=sink_d.ap()), writes=["sinkT"], dma=True, stream="c_sink")
        P.op("sp", lambda e: e.dma_start(out=bmT[:, :], in_=bm_d.ap()), writes=["bmT"], dma=True, stream="c_bm")
        P.op("dve", lambda e: e.memset(ones64[:, :], 1.0), writes=["ones64"])
        P.op("dve", lambda e: e.memset(mhalf[:, :], -0.5), writes=["mhalf"])
        for i in range(2):
            P.op("dve", lambda e, i=i: e.scalar_tensor_tensor(
                out=g8[:, i:i + 1], in0=nrm[:, 2 * i:2 * i + 1], scalar=8.0, in1=nrm[:, 2 * i + 1:2 * i + 2],
                op0=ALU.mult, op1=ALU.mult), reads=["nrm"], writes=["g8_%d" % i])
        P.op("act", lambda e: e.activation(out=esink[:, :], in_=sinkT[:, :], func=AF.Exp),
             reads=["sinkT"], writes=["esink"])

        with ExitStack() as es1:
            def sb1(name, shape, dt):
                return es1.enter_context(nc.sbuf_tensor(name, shape, dt))

            accN = sb1("accN", [128, NT], F32)
            accD = sb1("accD", [128, NT], F32)
            accDb = accD.bitcast(BF)
            xt = [accN[:, i * DM:(i + 1) * DM] for i in range(4)]
            hn = [accDb[:, i * DM:(i + 1) * DM] for i in range(4)]
            junk = accDb[:, 4 * DM:5 * DM]
            ssq = sb1("ssq", [128, 32], F32)
            rt = sb1("rt", [128, 32], F32)
            rstd = sb1("rstd", [128, 32], F32)
            def p0_a(t):
                b = t % 4
                P.op("sp", lambda e: e.dma_start(out=xt[b], in_=x_d.ap()[t * 128:(t + 1) * 128, :]),
                     writes=["xt%d" % b], dma=True, stream="xt%d" % b)
                P.op("act", lambda e: e.activation(out=junk, in_=xt[b], func=AF.Square, accum_out=ssq[:, t:t + 1]),
                     reads=["xt%d" % b], writes=["ssq%d" % t, "junk"])
                P.op("dve", lambda e: e.tensor_scalar(out=rt[:, t:t + 1], in0=ssq[:, t:t + 1], scalar1=1.0 / DM,
                                                      scalar2=EPS, op0=ALU.mult, op1=ALU.add),
                     reads=["ssq%d" % t], writes=["rt%d" % t])
                P.op("pool", lambda e: e.tensor_tensor(out=rstd[:, t:t + 1], in0=rt[:, t:t + 1], in1=mhalf[:, :],
                                                       op=ALU.pow),
                     reads=["rt%d" % t, "mhalf"], writes=["rstd%d" % t])

            def p0_b(t):
                b = t % 4
                P.op("act", lambda e: e.activation(out=hn[b], in_=xt[b], func=AF.Identity, scale=rstd[:, t:t + 1]),
                     reads=["xt%d" % b, "rstd%d" % t], writes=["hn%d" % b])
                _, _, ptok = bank(b)
                pbf = psbf[b // 2]
                pofs = (b % 2) * 1024

                def tr(e):
                    ins = None
                    for c in range(8):
                        ins = e.transpose(out=AP(pbf, pofs + c * 128, [[2048, 128], [1, 128]]),
                                          in_=AP(accDb, b * DM + c * 128, [[2 * NT, 128], [1, 128]]), identity=ident[:, :])
                    return ins
                P.op("pe", tr, reads=["hn%d" % b, "ident"], writes=[ptok])
                P.op("dve", lambda e: e.tensor_tensor(
                    out=AP(hT, t * 128, [[8 * NT, 128], [NT, 8], [1, 128]]),
                    in0=AP(pbf, pofs, [[2048, 128], [128, 8], [1, 128]]),
                    in1=AP(gT, 0, [[8, 128], [1, 8], [0, 128]]), op=ALU.mult),
                    reads=[ptok, "gT"], writes=["hT_%d" % (t // 4)])

            p0_a(0)
            p0_a(1)
            for t in range(32):
                if t + 2 < 32:
                    p0_a(t + 2)
                p0_b(t)

            wbuf = [sb1("wbuf%d" % i, [128, 8 * 512], BF) for i in range(2)]
            QT = sb1("QT", [128, NT], BF)
            KT = sb1("KT", [128, NT], BF)
            VT = sb1("VT", [128, 32 * 128], BF)
            gate = sb1("gate", [128, NT], BF)
            bA = sb1("bA", [128, 8 * 384], BF)
            bB = sb1("bB", [128, 24 * 256], BF)
            rawq = [sb1("rawq%d" % i, [128, 512], BF) for i in range(2)]
            rawk = [sb1("rawk%d" % i, [128, 512], BF) for i in range(2)]
            sqq = [sb1("sqq%d" % i, [128, 512], BF) for i in range(2)]
            sqk = [sb1("sqk%d" % i, [128, 512], BF) for i in range(2)]
            lnq = [sb1("lnq%d" % i, [128, 512], F32) for i in range(2)]
            lnk = [sb1("lnk%d" % i, [128, 512], F32) for i in range(2)]
            rq, rk = lnq, lnk
            PT = [sb1("PT%d" % i, [128, 2 * 384], BF) for i in range(5)]
            fln = [sb1("fln%d" % i, [128, 512], F32) for i in range(2)]
            frec = [sb1("frec%d" % i, [128, 512], F32) for i in range(2)]
            ft = [sb1("ft%d" % i, [128, 512], F32) for i in range(2)]
            yblk = [sb1("yblk%d" % i, [128, 512], BF) for i in range(2)]

            P.op("pool", lambda e: e.dma_start(out=AP(bA, 0, [[8 * 384, 128], [384, 8], [1, 384]]),
                                               in_=AP(bA_d, 0, [[384, 128], [128 * 384, 8], [1, 384]])),
                 writes=["bA"], dma=True, stream="c_bA")
            for g in range(3):
                P.op("pool", lambda e, g=g: e.dma_start(
                    out=AP(bB, g * 8 * 256, [[24 * 256, 128], [256, 8], [1, 256]]),
                    in_=AP(bB_d, g * 8 * 128 * 256, [[256, 128], [128 * 256, 8], [1, 256]])),
                    writes=["bB%d" % g], dma=True, stream="c_bB%d" % g)
                P.op("act", lambda e, g=g: e.activation(out=bB[:, g * 2048:(g + 1) * 2048],
                                                        in_=bB[:, g * 2048:(g + 1) * 2048], func=AF.Exp),
                     reads=["bB%d" % g], writes=["bB%d" % g])

            def load_w(ui):
                wp = ui % 2
                n = 512 if UNITS[ui]["gate"] else 384
                P.op("pool", lambda e: e.dma_start(
                    out=AP(wbuf[wp], 0, [[8 * 512, 128], [512, 8], [1, n]]),
                    in_=AP(wcat_d, ui * 512, [[WCOLS, 128], [128 * WCOLS, 8], [1, n]])),
                    writes=["wbuf%d" % wp], dma=True, stream="wbuf%d" % wp)

            rot = [0]

            def next_bank():
                k = rot[0] % 8
                rot[0] += 1
                return bank(k)

            fin_i = [0]
            yscr_last = {}

            def finalize_out(u, blk, num_ap, den_ap, num_tok, den_tok, den_bias):
                i = fin_i[0] % 2
                fin_i[0] += 1
                if den_bias is None:
                    P.op("act", lambda e: e.activation(out=fln[i][:, :], in_=den_ap, func=AF.Ln),
                         reads=[den_tok], writes=["fln%d" % i])
                else:
                    P.op("act", lambda e: e.activation(out=fln[i][:, :], in_=den_ap, func=AF.Ln, bias=den_bias),
                         reads=[den_tok, "esink"], writes=["fln%d" % i])
                P.op("act", lambda e: e.activation(out=frec[i][:, :], in_=fln[i][:, :], func=AF.Exp, scale=-1.0),
                     reads=["fln%d" % i], writes=["frec%d" % i])
                P.op("dve", lambda e: e.tensor_tensor(out=ft[i][:, :], in0=num_ap, in1=frec[i][:, :], op=ALU.mult),
                     reads=[num_tok, "frec%d" % i], writes=["ft%d" % i])
                P.op("pool", lambda e: e.tensor_tensor(out=yblk[i][:, :], in0=ft[i][:, :],
                                                       in1=gate[:, blk * 512:(blk + 1) * 512], op=ALU.mult),
                     reads=["ft%d" % i, "gate_%d" % blk], writes=["yblk%d" % i])
                ch = u["ychunk"]
                o = P.op("sp", lambda e: e.dma_start(out=AP(yscr_d, ch * 128 * NT + blk * 512, [[NT, 128], [1, 512]]),
                                                     in_=yblk[i][:, :]),
                         reads=["yblk%d" % i], dma=True, stream="yblk%d" % i)
                yscr_last[i] = o

            load_w(0)

            def do_unit(ui, u):
                wp = ui % 2
                D, W = u["D"], u["W"]
                L = NT // D
                if ui + 1 < len(UNITS):
                    load_w(ui + 1)
                wtok = "wbuf%d" % wp
                gi = 0 if u["kind"] == "A" else 1

                def tok_ap(tb, c):
                    if D == 16:
                        return AP(hT, c * NT + 2 * tb, [[8 * NT, 128], [1, 2], [16, 256]])
                    cls = (tb * 512) // L
                    m0 = (tb * 512) % L
                    return AP(hT, c * NT + m0 * D + cls, [[8 * NT, 128], [D, 512]])

                def nat_ap(tb, c):
                    return AP(hT, c * NT + tb * 512, [[8 * NT, 128], [1, 512]])

                RB = 512 // D

                def ps_perm(t_, o_):
                    return AP(t_, o_, [[1024, 128], [1, D], [D, RB]])

                def sb_perm(t_):
                    return AP(t_, 0, [[512, 128], [RB, D], [1, RB]])

                def dil_dst(t_, tb):
                    return AP(t_, RB * tb, [[NT, 128], [L, D], [1, RB]])

                reuse_kv = (u["kind"] == "A" and ui % 2 == 1)

                def tile_ap(ti, c):
                    cls = (ti * 128) // L
                    m0 = (ti * 128) % L
                    return AP(hT, c * NT + m0 * D + cls, [[8 * NT, 128], [D, 128]])

                st = {}

                def stage1(tb):
                    i2 = tb % 2
                    (tq, oq, kq), (tk, ok, kk), (tv, ov, kv) = next_bank(), next_bank(), next_bank()

                    def pq(e):
                        ins = None
                        for c in range(8):
                            ins = e.matmul(AP(tq, oq, [[1024, 128], [1, 512]]),
                                           AP(wbuf[wp], c * 512, [[8 * 512, 128], [1, 128]]), nat_ap(tb, c),
                                           start=(c == 0), stop=(c == 7))
                        return ins

                    def pk(e):
                        ins = None
                        for c in range(8):
                            ins = e.matmul(AP(tk, ok, [[1024, 128], [1, 512]]),
                                           AP(wbuf[wp], c * 512 + 128, [[8 * 512, 128], [1, 128]]), nat_ap(tb, c),
                                           start=(c == 0), stop=(c == 7))
                        return ins

                    def pv(e):
                        ins = None
                        for j in range(4):
                            for c in range(8):
                                ins = e.matmul(AP(tv, ov + j * 128, [[1024, 128], [1, 128]]),
                                               tile_ap(tb * 4 + j, c),
                                               AP(wbuf[wp], c * 512 + 256, [[8 * 512, 128], [1, 128]]),
                                               start=(c == 0), stop=(c == 7))
                        return ins
                    P.op("pe", pq, reads=[wtok, "hT_%d" % tb], writes=[kq])
                    if not reuse_kv:
                        P.op("pe", pk, reads=[wtok, "hT_%d" % tb], writes=[kk])
                        P.op("pe", pv, reads=[wtok] + (["hT_%d" % tb] if D == 1 else hT_all), writes=[kv])
                    P.op("act", lambda e: e.activation(out=sb_perm(rawq[i2]), in_=ps_perm(tq, oq),
                                                       func=AF.Copy), reads=[kq], writes=["rawq%d" % i2])
                    P.op("pool", lambda e: e.tensor_tensor(out=sqq[i2][:, :], in0=rawq[i2][:, :], in1=rawq[i2][:, :],
                                                           op=ALU.mult), reads=["rawq%d" % i2], writes=["sqq%d" % i2])
                    if not reuse_kv:
                        P.op("act", lambda e: e.activation(out=sb_perm(rawk[i2]), in_=ps_perm(tk, ok),
                                                           func=AF.Copy), reads=[kk], writes=["rawk%d" % i2])
                        P.op("dve", lambda e: e.tensor_copy(out=VT[:, tb * 512:(tb + 1) * 512],
                                                            in_=AP(tv, ov, [[1024, 128], [1, 512]])),
                             reads=[kv], writes=["VT_%d" % tb])
                        P.op("pool", lambda e: e.tensor_tensor(out=sqk[i2][:, :], in0=rawk[i2][:, :], in1=rawk[i2][:, :],
                                                               op=ALU.mult), reads=["rawk%d" % i2], writes=["sqk%d" % i2])

                def stage2(tb):
                    i2 = tb % 2
                    for (sq_, ln_, r_, raw_, dst, nm) in ((sqq, lnq, rq, rawq, QT, "q"), (sqk, lnk, rk, rawk, KT, "k")):
                        if nm == "k" and reuse_kv:
                            continue
                        tsq, osq, ksq = next_bank()
                        P.op("pe", lambda e, sq_=sq_, tsq=tsq, osq=osq: e.matmul(
                            AP(tsq, osq, [[1024, 128], [1, 512]]), bones[:, :], sq_[i2][:, :], start=True, stop=True),
                            reads=["s%s%d" % (nm, i2) if False else ("sq%s%d" % (nm, i2)), "bones"], writes=[ksq])
                        P.op("act", lambda e, ln_=ln_, tsq=tsq, osq=osq: e.activation(
                            out=ln_[i2][:, :], in_=AP(tsq, osq, [[1024, 128], [1, 512]]), func=AF.Ln, bias=64.0 * EPS),
                            reads=[ksq], writes=["ln%s%d" % (nm, i2)])
                        P.op("act", lambda e, ln_=ln_, r_=r_: e.activation(
                            out=r_[i2][:, :], in_=ln_[i2][:, :], func=AF.Exp, scale=-0.5),
                            reads=["ln%s%d" % (nm, i2)], writes=["ln%s%d" % (nm, i2)])
                        if nm == "q":
                            P.op("dve", lambda e, r_=r_, raw_=raw_, dst=dst: e.tensor_tensor(
                                out=dil_dst(dst, tb), in0=sb_perm(raw_[i2]), in1=sb_perm(r_[i2]), op=ALU.mult),
                                reads=["rawq%d" % i2, "lnq%d" % i2], writes=["QT_%d" % tb])
                        else:
                            P.op("dve", lambda e, r_=r_, raw_=raw_, dst=dst: e.scalar_tensor_tensor(
                                out=dil_dst(dst, tb), in0=sb_perm(raw_[i2]), scalar=g8[:, gi:gi + 1],
                                in1=sb_perm(r_[i2]), op0=ALU.mult, op1=ALU.mult),
                                reads=["rawk%d" % i2, "lnk%d" % i2, "g8_%d" % gi], writes=["KT_%d" % tb])

                stage1(0)
                for tb in range(1, 8):
                    stage1(tb)
                    stage2(tb - 1)
                stage2(7)
                if u["gate"]:
                    for tb in range(8):
                        tg, og, kg = next_bank()

                        def pg(e, tb=tb, tg=tg, og=og):
                            ins = None
                            for c in range(8):
                                ins = e.matmul(AP(tg, og, [[1024, 128], [1, 512]]),
                                               AP(wbuf[wp], c * 512 + 384, [[8 * 512, 128], [1, 128]]),
                                               AP(hT, c * NT + tb * 512, [[8 * NT, 128], [1, 512]]),
                                               start=(c == 0), stop=(c == 7))
                            return ins
                        P.op("pe", pg, reads=[wtok, "hT_%d" % tb], writes=[kg])
                        P.op("act", lambda e, tb=tb, tg=tg, og=og: e.activation(
                            out=gate[:, tb * 512:(tb + 1) * 512], in_=AP(tg, og, [[1024, 128], [1, 512]]), func=AF.Silu),
                            reads=[kg], writes=["gate_%d" % tb])

                QBS = min(512, L)
                NJ = 128 + 2 * W
                if u["kind"] == "A":
                    btile, bh0, nbh, btok = bA, 2 * u["ci"], 8, "bA"
                else:
                    btile, bh0, nbh, btok = bB, u["g"] * 8 + 2 * u["ci"], 24, "bB%d" % u["g"]
                pieces = []
                qbi = 0
                for cls in range(D):
                    for j in range(L // QBS):
                        q0 = j * QBS
                        lst = []
                        for kt in range(L // 128):
                            qa = max(128 * kt - W, q0, 0)
                            qb_ = min(128 * kt + 128 + W, q0 + QBS, L)
                            if qb_ > qa:
                                lst.append((kt, qa, qb_))
                        for n, (kt, qa, qb_) in enumerate(lst):
                            pieces.append(dict(cls=cls, j=j, q0=q0, kt=kt, qa=qa, qb=qb_, first=(n == 0),
                                               last=(n == len(lst) - 1), qbi=qbi))
                        qbi += 1
                allQ = ["QT_%d" % t for t in range(8)]
                allK = ["KT_%d" % t for t in range(8)]
                allV = ["VT_%d" % t for t in range(8)]

                isB = (u["kind"] == "B")
                NSLOT = 2
                DEPTH = 2 if isB else 1
                NPT = 5

                def slot(pi):
                    sl = pi % NSLOT
                    return ps[sl], 0, ["ps%da" % sl, "ps%db" % sl]

                def emit_S(pi, pc):
                    t_, co, toks = slot(pi)
                    nq = pc["qb"] - pc["qa"]
                    nk = pc["cls"] * L + pc["kt"] * 128
                    nqo = pc["cls"] * L + pc["qa"]
                    j0 = pc["qa"] - (128 * pc["kt"] - W)

                    def f(e):
                        ins = None
                        for h in range(2):
                            ins = e.matmul(AP(t_, h * 512 + co, [[1024, 128], [1, nq]]),
                                           AP(KT, h * 64 * NT + nk, [[NT, 64], [1, 128]]),
                                           AP(QT, h * 64 * NT + nqo, [[NT, 64], [1, nq]]), start=True, stop=isB)
                        if not isB:
                            for h in range(2):
                                ins = e.matmul(AP(t_, h * 512 + co, [[1024, 128], [1, nq]]), ident[:, :],
                                               AP(btile, (bh0 + h) * NJ + j0, [[nbh * NJ, 128], [1, nq]]),
                                               start=False, stop=True)
                        return ins
                    if D == 1:
                        qk_tok = ["QT_%d" % t for t in range(pc["qa"] // 512, (pc["qb"] - 1) // 512 + 1)] + \
                                 ["KT_%d" % (pc["kt"] // 4)]
                    else:
                        qk_tok = allQ + allK
                    P.op("pe", f, reads=qk_tok + ([] if isB else [btok]) + ["ident"], writes=toks)

                def emit_E(pi, pc):
                    t_, co, toks = slot(pi)
                    nq = pc["qb"] - pc["qa"]
                    p3 = pi % NPT
                    j0 = pc["qa"] - (128 * pc["kt"] - W)
                    P.op("act", lambda e: e.activation(out=AP(PT[p3], 0, [[768, 128], [384, 2], [1, nq]]),
                                                       in_=AP(t_, co, [[1024, 128], [512, 2], [1, nq]]), func=AF.Exp),
                         reads=toks, writes=["PT%d" % p3])
                    if isB:
                        P.op("dve", lambda e: e.tensor_tensor(
                            out=AP(PT[p3], 0, [[768, 128], [384, 2], [1, nq]]),
                            in0=AP(PT[p3], 0, [[768, 128], [384, 2], [1, nq]]),
                            in1=AP(btile, bh0 * NJ + j0, [[nbh * NJ, 128], [NJ, 2], [1, nq]]), op=ALU.mult),
                            reads=["PT%d" % p3, btok], writes=["PT%d" % p3])

                def emit_PV(pi, pc):
                    nq = pc["qb"] - pc["qa"]
                    p3 = pi % NPT
                    acc = 2 + pc["qbi"] % 2
                    qc = pc["qa"] - pc["q0"]
                    tile = (pc["cls"] * L) // 128 + pc["kt"]

                    def f(e):
                        ins = None
                        for h in range(2):
                            ins = e.matmul(AP(ps[acc], h * 64 * 1024 + qc, [[1024, 64], [1, nq]]),
                                           AP(VT, tile * 128 + h * 64, [[32 * 128, 128], [1, 64]]),
                                           AP(PT[p3], h * 384, [[768, 128], [1, nq]]),
                                           start=pc["first"], stop=pc["last"], skip_group_check=True,
                                           tile_position=(0, h * 64))
                        for h in range(2):
                            ins = e.matmul(AP(ps[acc], h * 64 * 1024 + 512 + qc, [[1024, 64], [1, nq]]),
                                           ones64[:, :],
                                           AP(PT[p3], h * 384, [[768, 128], [1, nq]]),
                                           start=pc["first"], stop=pc["last"], skip_group_check=True,
                                           tile_position=(0, h * 64))
                        return ins
                    v_tok = ["VT_%d" % (pc["kt"] // 4)] if D == 1 else allV
                    P.op("pe", f, reads=["PT%d" % p3, "ones64"] + v_tok, writes=["ps%da" % acc, "ps%db" % acc])

                def emit_fin(pc):
                    acc = 2 + pc["qbi"] % 2
                    ntok, dtok = "ps%da" % acc, "ps%db" % acc
                    num_ap = AP(ps[acc], 0, [[1024, 128], [1, QBS]])
                    den_ap = AP(ps[acc], 512, [[1024, 128], [1, QBS]])
                    if u["kind"] == "A":
                        finalize_out(u, pc["j"], num_ap, den_ap, ntok, dtok, esink[:, u["ci"]:u["ci"] + 1])
                        return
                    if D == 1:
                        off, dims = pc["q0"], [[1, QBS]]
                    else:
                        off, dims = pc["q0"] * D + pc["cls"], [[D, QBS]]
                    vN = AP(accN, off, [[NT, 128]] + dims)
                    vD = AP(accD, off, [[NT, 128]] + dims)
                    if u["g"] == 0:
                        P.op("dve", lambda e: e.tensor_copy(out=vN, in_=num_ap), reads=[ntok], writes=["accN"])
                        P.op("act", lambda e: e.activation(out=vD, in_=den_ap, func=AF.Copy), reads=[dtok], writes=["accD"])
                    else:
                        P.op("dve", lambda e: e.tensor_tensor(out=vN, in0=vN, in1=num_ap, op=ALU.add),
                             reads=[ntok, "accN"], writes=["accN"])
                        P.op("dve", lambda e: e.tensor_tensor(out=vD, in0=vD, in1=den_ap, op=ALU.add),
                             reads=[dtok, "accD"], writes=["accD"])

                for pi in range(min(DEPTH, len(pieces))):
                    emit_S(pi, pieces[pi])
                pending = []
                for pi, pc in enumerate(pieces):
                    emit_E(pi, pc)
                    if pi + DEPTH < len(pieces):
                        emit_S(pi + DEPTH, pieces[pi + DEPTH])
                    emit_PV(pi, pc)
                    while pending and pending[0][0] <= pi:
                        emit_fin(pending.pop(0)[1])
                    if pc["last"]:
                        pending.append((pi + 1, pc))
                while pending:
                    emit_fin(pending.pop(0)[1])
                if u["kind"] == "B" and u["g"] == 2:
                    for blk in range(8):
                        finalize_out(u, blk, accN[:, blk * 512:(blk + 1) * 512], accD[:, blk * 512:(blk + 1) * 512],
                                     "accN", "accD", None)

            for ui, u in enumerate(UNITS):
                do_unit(ui, u)
            o_end = P.op("sp", None)
            for o in yscr_last.values():
                o_end.deps.append(o)
            if debug:
                dd = []
                dd.append(P.op("sp", lambda e: e.dma_start(out=hTd.ap(), in_=hT[:, :]), reads=hT_all, dma=True, stream="dbg0"))
                dd.append(P.op("sp", lambda e: e.dma_start(out=accNd.ap(), in_=accN[:, :]), reads=["accN"], dma=True, stream="dbg1"))
                dd.append(P.op("sp", lambda e: e.dma_start(out=accDd.ap(), in_=accD[:, :]), reads=["accD"], dma=True, stream="dbg2"))
                dd.append(P.op("sp", lambda e: e.dma_start(out=QTd.ap(), in_=QT[:, :]), reads=["QT_%d" % t for t in range(8)], dma=True, stream="dbg3"))
                dd.append(P.op("sp", lambda e: e.dma_start(out=KTd.ap(), in_=KT[:, :]), reads=["KT_%d" % t for t in range(8)], dma=True, stream="dbg4"))
                dd.append(P.op("sp", lambda e: e.dma_start(out=VTd.ap(), in_=VT[:, :]), reads=["VT_%d" % t for t in range(8)], dma=True, stream="dbg5"))
                dd.append(P.op("sp", lambda e: e.dma_start(out=gated.ap(), in_=gate[:, :]), reads=["gate_%d" % t for t in range(8)], dma=True, stream="dbg6"))
                o_e2 = P.op("sp", None)
                o_e2.deps.extend(dd)
            with nc.Block() as block:
                P.emit(block)

        with ExitStack() as es2:
            def sb2(name, shape, dt):
                return es2.enter_context(nc.sbuf_tensor(name, shape, dt))
            Wmg = sb2("Wmg", [128, 8 * 2048], BF)
            Wa = sb2("Wa", [128, 4 * 1024], BF)
            Wb = sb2("Wb", [128, 4 * 1024], BF)
            Wo = sb2("Wo", [128, 8 * 1024], BF)
            ybuf = [sb2("ybuf%d" % i, [128, 8 * 512], BF) for i in range(2)]
            g0 = [sb2("g0_%d" % i, [128, 512], BF) for i in range(2)]
            g1 = [sb2("g1_%d" % i, [128, 512], BF) for i in range(2)]
            t0 = [sb2("t0_%d" % i, [128, 512], F32) for i in range(2)]
            t1 = [sb2("t1_%d" % i, [128, 512], F32) for i in range(2)]
            mT = [sb2("mT%d" % i, [128, 8 * 512], BF) for i in range(2)]
            NXR = 3
            xres = [sb2("xres%d" % i, [128, 512], F32) for i in range(NXR)]
            yo = [sb2("yo%d" % i, [128, 512], F32) for i in range(NXR)]

            P.op("pool", lambda e: e.dma_start(out=AP(Wa, 0, [[4 * 1024, 128], [1024, 4], [1, 1024]]),
                                               in_=AP(wa_d, 0, [[DM, 128], [128 * DM, 4], [1, 1024]])),
                 writes=["Wa"], dma=True, stream="Wa")
            P.op("pool", lambda e: e.dma_start(out=AP(Wb, 0, [[4 * 1024, 128], [1024, 4], [1, 1024]]),
                                               in_=AP(wb_d, 0, [[DM, 128], [128 * DM, 4], [1, 1024]])),
                 writes=["Wb"], dma=True, stream="Wb")
            for qd in range(4):
                for half in range(2):
                    P.op("pool", lambda e, half=half, qd=qd: e.dma_start(
                        out=AP(Wmg, half * 1024 + qd * 256, [[8 * 2048, 128], [2048, 8], [1, 256]]),
                        in_=AP(wcat_d, MG_OFF + half * 1024 + qd * 256, [[WCOLS, 128], [128 * WCOLS, 8], [1, 256]])),
                        writes=["Wmg%d_%d" % (half, qd)], dma=True, stream=("Wa", "Wb")[half])
            P.op("pool", lambda e: e.dma_start(out=AP(Wo, 0, [[8 * 1024, 128], [1024, 8], [1, 1024]]),
                                               in_=AP(wo_d, 0, [[DM, 128], [128 * DM, 8], [1, 1024]])),
                 writes=["Wo"], dma=True, stream="Wa")
            rot2 = [0]

            def nb2():
                k = rot2[0] % 8
                rot2[0] += 1
                return bank(k)
            out_dmas = []
            oi = [0]

            def emit_xload(k):
                tb_, rem = divmod(k, 8)
                tt, nh = divmod(rem, 2)
                i3 = k % NXR
                r0 = tb_ * 512 + tt * 128
                P.op("sp", lambda e: e.dma_start(out=xres[i3][:, :], in_=x_d.ap()[r0:r0 + 128, nh * 512:(nh + 1) * 512]),
                     writes=["xres%d" % i3], dma=True, stream="xres%d" % i3)

            def emit_out(tb):
                mp = tb % 2
                for tt in range(4):
                    for nh in range(2):
                        k = tb * 8 + tt * 2 + nh
                        i3 = k % NXR
                        if k == 0:
                            for kk in range(NXR - 1):
                                emit_xload(kk)
                        to, oo, ko = nb2()
                        r0 = tb * 512 + tt * 128

                        def po(e, tt=tt, nh=nh, to=to, oo=oo):
                            ins = None
                            for c in range(8):
                                ins = e.matmul(AP(to, oo, [[1024, 128], [1, 512]]),
                                               AP(mT[mp], c * 512 + tt * 128, [[8 * 512, 128], [1, 128]]),
                                               AP(Wo, c * 1024 + nh * 512, [[8 * 1024, 128], [1, 512]]),
                                               start=(c == 0), stop=(c == 7))
                            return ins
                        P.op("pe", po, reads=["mT%d_%d" % (mp, c) for c in range(8)] + ["Wo"], writes=[ko])
                        P.op("dve", lambda e, i3=i3, to=to, oo=oo: e.tensor_tensor(
                            out=yo[i3][:, :], in0=AP(to, oo, [[1024, 128], [1, 512]]), in1=xres[i3][:, :], op=ALU.add),
                            reads=[ko, "xres%d" % i3], writes=["yo%d" % i3])
                        if k + NXR - 1 < 64:
                            emit_xload(k + NXR - 1)
                        o = P.op("sp", lambda e, i3=i3, r0=r0, nh=nh: e.dma_start(
                            out=y_d.ap()[r0:r0 + 128, nh * 512:(nh + 1) * 512], in_=yo[i3][:, :]),
                            reads=["yo%d" % i3], dma=True, stream="yo%d" % i3)
                        out_dmas.append(o)

            gi_ = [0]
            for tb in range(8):
                yp = tb % 2
                mp = tb % 2
                P.op("pool", lambda e, tb=tb, yp=yp: e.dma_start(
                    out=AP(ybuf[yp], 0, [[8 * 512, 128], [512, 8], [1, 512]]),
                    in_=AP(yscr_d, tb * 512, [[NT, 128], [128 * NT, 8], [1, 512]])),
                    writes=["ybuf%d" % yp], dma=True, stream="ybuf%d" % yp)
                for c in range(8):
                    i2 = gi_[0] % 2
                    gi_[0] += 1
                    (tg0, og0, kg0), (tg1, og1, kg1), (ta, oa, ka), (tbb, obb, kbb) = nb2(), nb2(), nb2(), nb2()

                    def pgate(e, br, tg, og, c=c, tb=tb):
                        ins = None
                        for k in range(8):
                            ins = e.matmul(AP(tg, og, [[1024, 128], [1, 512]]),
                                           AP(Wmg, k * 2048 + br * 1024 + c * 128, [[8 * 2048, 128], [1, 128]]),
                                           AP(hT, k * NT + tb * 512, [[8 * NT, 128], [1, 512]]),
                                           start=(k == 0), stop=(k == 7))
                        return ins

                    def pbr(e, Wx, yoff, tx, ox, c=c, yp=yp):
                        ins = None
                        for k in range(4):
                            ins = e.matmul(AP(tx, ox, [[1024, 128], [1, 512]]),
                                           AP(Wx, k * 1024 + c * 128, [[4 * 1024, 128], [1, 128]]),
                                           AP(ybuf[yp], (yoff + k) * 512, [[8 * 512, 128], [1, 512]]),
                                           start=(k == 0), stop=(k == 3))
                        return ins
                    P.op("pe", lambda e, tg0=tg0, og0=og0, f=pgate: f(e, 0, tg0, og0), reads=["Wmg0_%d" % (c // 2), "hT_%d" % tb], writes=[kg0])
                    P.op("pe", lambda e, tg1=tg1, og1=og1, f=pgate: f(e, 1, tg1, og1), reads=["Wmg1_%d" % (c // 2), "hT_%d" % tb], writes=[kg1])
                    P.op("pe", lambda e, ta=ta, oa=oa, f=pbr: f(e, Wa, 0, ta, oa), reads=["Wa", "ybuf%d" % yp], writes=[ka])
                    P.op("pe", lambda e, tbb=tbb, obb=obb, f=pbr: f(e, Wb, 4, tbb, obb), reads=["Wb", "ybuf%d" % yp], writes=[kbb])
                    P.op("act", lambda e, tg0=tg0, og0=og0, c=c, i2=i2: e.activation(
                        out=g0[i2][:, :], in_=AP(tg0, og0, [[1024, 128], [1, 512]]), func=AF.Sigmoid, bias=bmT[:, c:c + 1]),
                        reads=[kg0, "bmT"], writes=["g0_%d" % i2])
                    P.op("act", lambda e, tg1=tg1, og1=og1, c=c, i2=i2: e.activation(
                        out=g1[i2][:, :], in_=AP(tg1, og1, [[1024, 128], [1, 512]]), func=AF.Sigmoid,
                        bias=bmT[:, 8 + c:9 + c]),
                        reads=[kg1, "bmT"], writes=["g1_%d" % i2])
                    P.op("dve", lambda e, ta=ta, oa=oa, i2=i2: e.tensor_tensor(
                        out=t0[i2][:, :], in0=AP(ta, oa, [[1024, 128], [1, 512]]), in1=g0[i2][:, :], op=ALU.mult),
                        reads=[ka, "g0_%d" % i2], writes=["t0_%d" % i2])
                    P.op("dve", lambda e, tbb=tbb, obb=obb, i2=i2: e.tensor_tensor(
                        out=t1[i2][:, :], in0=AP(tbb, obb, [[1024, 128], [1, 512]]), in1=g1[i2][:, :], op=ALU.mult),
                        reads=[kbb, "g1_%d" % i2], writes=["t1_%d" % i2])
                    P.op("pool", lambda e, c=c, i2=i2, mp=mp: e.tensor_tensor(
                        out=mT[mp][:, c * 512:(c + 1) * 512], in0=t0[i2][:, :], in1=t1[i2][:, :], op=ALU.add),
                        reads=["t0_%d" % i2, "t1_%d" % i2], writes=["mT%d_%d" % (mp, c)])
                if tb >= 1:
                    emit_out(tb - 1)
            emit_out(7)
            o_end = P.op("sp", None)
            o_end.deps.extend(out_dmas[-6:])
            with nc.Block() as block:
                P.emit(block)
    return nc


_NC_CACHE = {}


def _prep_shared(inputs):
    w_in = np.asarray(inputs["w_in"], np.float32)[0]
    cols = []
    for u in UNITS:
        cols += _unit_cols(u)
    cols += list(range(6400, 8448))
    wcat = np.ascontiguousarray(w_in[:, np.asarray(cols)])
    gain = np.asarray(inputs["norm_gain"], np.float32)[0]
    gT = np.ascontiguousarray(gain.reshape(8, 128).T)
    nrm = np.stack([np.tile(np.asarray(inputs[k], np.float32)[0], 2)
                    for k in ("q_norm_a", "k_norm_a", "q_norm_b", "k_norm_b")], axis=1)
    sink = np.asarray(inputs["sink_a"], np.float32)[0]
    sinkT = np.stack([np.repeat(sink[2 * ci:2 * ci + 2], 64) for ci in range(4)], axis=1)
    bm = np.asarray(inputs["b_merge"], np.float32)[0]
    bmT = np.ascontiguousarray(bm.reshape(2, 8, 128).transpose(2, 0, 1).reshape(128, 16))
    rb = np.asarray(inputs["rel_bias"], np.float32)
    p = np.arange(128)[:, None]
    j = np.arange(384)[None, :]
    rel = p - j + 128
    idx = _t5_bucket(rel)
    valid = np.abs(rel) <= 128
    biasA = np.stack([np.where(valid, rb[idx, h], np.float32(NEG)) for h in range(8)]).astype(np.float32)
    j = np.arange(256)[None, :]
    rel = p - j + 64
    valid = np.abs(rel) <= 64
    bl = []
    for g, D in enumerate((1, 4, 16)):
        idx = _t5_bucket(rel * D)
        for h in range(8):
            bl.append(np.where(valid, rb[idx, 8 + 8 * g + h], np.float32(NEG)))
    biasB = np.stack(bl).astype(np.float32)
    ident = np.eye(128, dtype=np.float32)
    bones = np.kron(np.eye(2, dtype=np.float32), np.ones((64, 64), np.float32))
    return dict(wcat=wcat, w_a=np.ascontiguousarray(np.asarray(inputs["w_branch_a"], np.float32)[0]),
                w_b=np.ascontiguousarray(np.asarray(inputs["w_branch_b"], np.float32)[0]),
                w_o=np.ascontiguousarray(np.asarray(inputs["w_out"], np.float32)[0]),
                gT=gT, nrm=np.ascontiguousarray(nrm), sinkT=np.ascontiguousarray(sinkT), bmT=bmT,
                biasA=np.ascontiguousarray(biasA), biasB=np.ascontiguousarray(biasB), ident=ident, bones=bones)


def kernel(**inputs):
    x = np.asarray(inputs["x"], np.float32)
    shared = _prep_shared(inputs)
    if "nc" not in _NC_CACHE:
        _NC_CACHE["nc"] = build_nc()
    nc = _NC_CACHE["nc"]
    in_maps = []
    for b in range(8):
        m = dict(shared)
        m["x"] = np.ascontiguousarray(x[b])
        in_maps.append(m)
    res = run_bass_kernel_spmd(nc, in_maps, core_ids=list(range(8)))
    return np.stack([np.asarray(r["y"], np.float32) for r in res.results], axis=0)
```

```python
import math
import numpy as np
import concourse.bass as bass
import concourse.mybir as mybir
from concourse.bass_utils import run_bass_kernel_spmd

F32 = mybir.dt.float32
BF = mybir.dt.bfloat16
AF = mybir.ActivationFunctionType
ALU = mybir.AluOpType

NT = 4096
DM = 1024
EPS = 1e-6
NEG = -30000.0
MG_OFF = 16 * 512
WCOLS = MG_OFF + 2048

UNITS = []
for j in range(2):
    for i in range(2):
        ci = 2 * j + i
        UNITS.append(dict(kind="A", D=1, W=128, ci=ci, gate=True, ychunk=ci, g=0,
                          q=(4 * j + 2 * i) * 64, k=[512 + 64 * j, 512 + 64 * j],
                          v=[640 + 64 * j, 640 + 64 * j], gt=5376 + ci * 128))
for hp in range(4):
    for g, D in enumerate((1, 4, 16)):
        UNITS.append(dict(kind="B", D=D, W=64, ci=hp, gate=(g == 2), ychunk=4 + hp, g=g,
                          q=768 + g * 512 + hp * 128,
                          k=[2304 + g * 512 + hp * 128, 2304 + g * 512 + hp * 128 + 64],
                          v=[3840 + g * 512 + hp * 128, 3840 + g * 512 + hp * 128 + 64],
                          gt=5888 + hp * 128))


def _unit_cols(u):
    cols = list(range(u["q"], u["q"] + 128))
    cols += list(range(u["k"][0], u["k"][0] + 64)) + list(range(u["k"][1], u["k"][1] + 64))
    cols += list(range(u["v"][0], u["v"][0] + 64)) + list(range(u["v"][1], u["v"][1] + 64))
    cols += list(range(u["gt"], u["gt"] + 128))
    return cols


def _t5_bucket(rel):
    rel = np.asarray(rel, dtype=np.int64)
    half, max_exact = 16, 8
    ret = (rel > 0).astype(np.int64) * half
    n = np.abs(rel)
    nf = np.maximum(n, max_exact).astype(np.float32)
    large = max_exact + (np.log(nf / np.float32(max_exact)) / np.float32(math.log(1024 / max_exact))
                         * np.float32(half - max_exact)).astype(np.int32)
    large = np.minimum(large, half - 1)
    return ret + np.where(n < max_exact, n, large)


class Op:
    __slots__ = ("eng", "fn", "deps", "pos", "inc", "ms", "dma", "stream", "ordn")

    def __init__(self, eng, fn, dma=False, stream=None):
        self.eng, self.fn, self.dma, self.stream = eng, fn, dma, stream
        self.deps = []
        self.inc = False
        self.ms = 0
        self.ordn = 0


class Prog:
    ENG = ("pe", "act", "dve", "pool", "sp")

    def __init__(self, nc, sems, dsems):
        self.nc = nc
        self.sems = sems
        self.dsems = dsems
        self.stream_sem = {}
        self.stream_cnt = {}
        self.stream_last = {}
        self.ms_cnt = {e: 0 for e in self.ENG}
        self.lw = {}
        self.rd = {}
        self.seen = {e: {} for e in self.ENG}
        self.alias = {}
        self.reset()

    def reset(self):
        self.ops = {e: [] for e in self.ENG}

    def _expand(self, toks):
        out = []
        for t in toks:
            out.extend(self.alias.get(t, (t,)))
        return out

    def op(self, eng, fn, reads=(), writes=(), dma=False, stream=None):
        o = Op(eng, fn, dma, stream)
        reads = self._expand(reads)
        writes = self._expand(writes)
        deps = []
        for t in reads:
            w = self.lw.get(t)
            if w is not None:
                deps.append(w)
        for t in writes:
            w = self.lw.get(t)
            if w is not None:
                deps.append(w)
            for r in self.rd.get(t, {}).values():
                deps.append(r)
        if dma:
            if stream not in self.stream_sem:
                self.stream_sem[stream] = self.dsems.pop()
                self.stream_cnt[stream] = 0
            prev = self.stream_last.get(stream)
            if prev is not None:
                deps.append(prev)
            self.stream_cnt[stream] += 1
            o.ordn = self.stream_cnt[stream]
            self.stream_last[stream] = o
        for d in deps:
            if d is o:
                continue
            if (not d.dma) and d.eng == "pe" and eng == "pe" and not dma:
                continue
            o.deps.append(d)
            d.inc = True
        for t in reads:
            key = ("dma", stream) if dma else eng
            self.rd.setdefault(t, {})[key] = o
        for t in writes:
            self.lw[t] = o
            self.rd[t] = {}
        self.ops[eng].append(o)
        return o

    def emit(self, block):
        nc = self.nc
        for e in self.ENG:
            for o in self.ops[e]:
                if o.inc and not o.dma:
                    self.ms_cnt[e] += 1
                    o.ms = self.ms_cnt[e]
        prog = self

        def run(e, engobj):
            seen = prog.seen[e]
            for o in prog.ops[e]:
                need = {}
                for d in o.deps:
                    if d.dma:
                        s = prog.stream_sem[d.stream]
                        v = 16 * d.ordn
                    else:
                        s = prog.sems[d.eng]
                        v = d.ms
                    key = id(s)
                    if need.get(key, (None, 0))[1] < v:
                        need[key] = (s, v)
                for key, (s, v) in need.items():
                    if seen.get(key, 0) < v:
                        engobj.wait_ge(s, v)
                        seen[key] = v
                if o.fn is None:
                    continue
                ins = o.fn(engobj)
                if o.dma:
                    ins.then_inc(prog.stream_sem[o.stream], 16)
                elif o.inc:
                    ins.then_inc(prog.sems[e], 1)

        @block.tensor
        def _(eng):
            run("pe", eng)

        @block.scalar
        def _(eng):
            run("act", eng)

        @block.vector
        def _(eng):
            run("dve", eng)

        @block.gpsimd
        def _(eng):
            run("pool", eng)

        @block.sync
        def _(eng):
            run("sp", eng)

        self.reset()


hT_all = ["hT_%d" % i for i in range(8)]


def AP(t, off, dims):
    return bass.AP(t, off, [list(d) for d in dims])


def build_nc(debug=None):
    nc = bass.Bass("TRN2", target_bir_lowering=False)
    dt_in = lambda n, s: nc.dram_tensor(n, s, F32, kind="ExternalInput")
    x_d = dt_in("x", [NT, DM])
    wcat_d = dt_in("wcat", [DM, WCOLS])
    wa_d = dt_in("w_a", [512, DM])
    wb_d = dt_in("w_b", [512, DM])
    wo_d = dt_in("w_o", [DM, DM])
    gT_d = dt_in("gT", [128, 8])
    nrm_d = dt_in("nrm", [128, 4])
    sink_d = dt_in("sinkT", [128, 4])
    bm_d = dt_in("bmT", [128, 16])
    bA_d = dt_in("biasA", [8, 128, 384])
    bB_d = dt_in("biasB", [24, 128, 256])
    id_d = dt_in("ident", [128, 128])
    bo_d = dt_in("bones", [128, 128])
    y_d = nc.dram_tensor("y", [NT, DM], F32, kind="ExternalOutput")
    yscr_d = nc.dram_tensor("yscr", [8, 128, NT], BF, kind="ExternalOutput" if debug else "Internal")
    if debug:
        hTd = nc.dram_tensor("hT_dbg", [128, 8 * NT], BF, kind="ExternalOutput")
        accNd = nc.dram_tensor("accN_dbg", [128, NT], F32, kind="ExternalOutput")
        accDd = nc.dram_tensor("accD_dbg", [128, NT], F32, kind="ExternalOutput")
        QTd = nc.dram_tensor("QT_dbg", [128, NT], BF, kind="ExternalOutput")
        KTd = nc.dram_tensor("KT_dbg", [128, NT], BF, kind="ExternalOutput")
        VTd = nc.dram_tensor("VT_dbg", [128, NT], BF, kind="ExternalOutput")
        gated = nc.dram_tensor("gate_dbg", [128, NT], BF, kind="ExternalOutput")

    from contextlib import ExitStack
    with ExitStack() as es:
        def sb(name, shape, dt):
            return es.enter_context(nc.sbuf_tensor(name, shape, dt))

        sems = {e: es.enter_context(nc.semaphore("s_" + e)) for e in ("pe", "act", "dve", "pool")}
        dsems = [es.enter_context(nc.semaphore("d%d" % i)) for i in range(40)]
        P = Prog(nc, sems, dsems)
        ps = [es.enter_context(nc.psum_tensor("ps%d" % i, [128, 1024], F32)) for i in range(4)]
        psbf = [p.bitcast(BF) for p in ps]
        for k_ in range(4):
            for h_ in "ab":
                P.alias["ps%d%s" % (k_, h_)] = ("ps%d%s_0" % (k_, h_), "ps%d%s_1" % (k_, h_))

        def bank(k):
            return ps[k // 2], (k % 2) * 512, "ps%d%s" % (k // 2, "ab"[k % 2])

        hT = sb("hT", [128, 8 * NT], BF)
        ident = sb("ident_s", [128, 128], BF)
        bones = sb("bones_s", [128, 128], BF)
        ones64 = sb("ones64", [128, 64], BF)
        gT = sb("gT_s", [128, 8], F32)
        nrm = sb("nrm_s", [128, 4], F32)
        g8 = sb("g8", [128, 2], F32)
        sinkT = sb("sink_s", [128, 4], F32)
        esink = sb("esink", [128, 4], F32)
        bmT = sb("bm_s", [128, 16], F32)
        mhalf = sb("mhalf", [128, 1], F32)

        P.op("pool", lambda e: e.dma_start(out=ident[:, :], in_=id_d.ap()), writes=["ident"], dma=True, stream="c_id")
        P.op("pool", lambda e: e.dma_start(out=bones[:, :], in_=bo_d.ap()), writes=["bones"], dma=True, stream="c_bo")
        P.op("sp", lambda e: e.dma_start(out=gT[:, :], in_=gT_d.ap()), writes=["gT"], dma=True, stream="c_gT")
        P.op("sp", lambda e: e.dma_start(out=nrm[:, :], in_=nrm_d.ap()), writes=["nrm"], dma=True, stream="c_nrm")
        P.op("sp", lambda e: e.dma_start(out=sinkT[:, :], in_=sink_d.ap()), writes=["sinkT"], dma=True, stream="c_sink")
        P.op("sp", lambda e: e.dma_start(out=bmT[:, :], in_=bm_d.ap()), writes=["bmT"], dma=True, stream="c_bm")
        P.op("dve", lambda e: e.memset(ones64[:, :], 1.0), writes=["ones64"])
        P.op("dve", lambda e: e.memset(mhalf[:, :], -0.5), writes=["mhalf"])
        for i in range(2):
            P.op("dve", lambda e, i=i: e.scalar_tensor_tensor(
                out=g8[:, i:i + 1], in0=nrm[:, 2 * i:2 * i + 1], scalar=8.0, in1=nrm[:, 2 * i + 1:2 * i + 2],
                op0=ALU.mult, op1=ALU.mult), reads=["nrm"], writes=["g8_%d" % i])
        P.op("act", lambda e: e.activation(out=esink[:, :], in_=sinkT[:, :], func=AF.Exp),
             reads=["sinkT"], writes=["esink"])

        with ExitStack() as es1:
            def sb1(name, shape, dt):
                return es1.enter_context(nc.sbuf_tensor(name, shape, dt))

            accN = sb1("accN", [128, NT], F32)
            accD = sb1("accD", [128, NT], F32)
            accDb = accD.bitcast(BF)
            xt = [accN[:, i * DM:(i + 1) * DM] for i in range(4)]
            hn = [accDb[:, i * DM:(i + 1) * DM] for i in range(4)]
            junk = accDb[:, 4 * DM:5 * DM]
            ssq = sb1("ssq", [128, 32], F32)
            rt = sb1("rt", [128, 32], F32)
            rstd = sb1("rstd", [128, 32], F32)
            def p0_a(t):
                b = t % 4
                P.op("sp", lambda e: e.dma_start(out=xt[b], in_=x_d.ap()[t * 128:(t + 1) * 128, :]),
                     writes=["xt%d" % b], dma=True, stream="xt%d" % b)
                P.op("act", lambda e: e.activation(out=junk, in_=xt[b], func=AF.Square, accum_out=ssq[:, t:t + 1]),
                     reads=["xt%d" % b], writes=["ssq%d" % t, "junk"])
                P.op("dve", lambda e: e.tensor_scalar(out=rt[:, t:t + 1], in0=ssq[:, t:t + 1], scalar1=1.0 / DM,
                                                      scalar2=EPS, op0=ALU.mult, op1=ALU.add),
                     reads=["ssq%d" % t], writes=["rt%d" % t])
                P.op("pool", lambda e: e.tensor_tensor(out=rstd[:, t:t + 1], in0=rt[:, t:t + 1], in1=mhalf[:, :],
                                                       op=ALU.pow),
                     reads=["rt%d" % t, "mhalf"], writes=["rstd%d" % t])

            def p0_b(t):
                b = t % 4
                P.op("act", lambda e: e.activation(out=hn[b], in_=xt[b], func=AF.Identity, scale=rstd[:, t:t + 1]),
                     reads=["xt%d" % b, "rstd%d" % t], writes=["hn%d" % b])
                _, _, ptok = bank(b)
                pbf = psbf[b // 2]
                pofs = (b % 2) * 1024

                def tr(e):
                    ins = None
                    for c in range(8):
                        ins = e.transpose(out=AP(pbf, pofs + c * 128, [[2048, 128], [1, 128]]),
                                          in_=AP(accDb, b * DM + c * 128, [[2 * NT, 128], [1, 128]]), identity=ident[:, :])
                    return ins
                P.op("pe", tr, reads=["hn%d" % b, "ident"], writes=[ptok])
                P.op("dve", lambda e: e.tensor_tensor(
                    out=AP(hT, t * 128, [[8 * NT, 128], [NT, 8], [1, 128]]),
                    in0=AP(pbf, pofs, [[2048, 128], [128, 8], [1, 128]]),
                    in1=AP(gT, 0, [[8, 128], [1, 8], [0, 128]]), op=ALU.mult),
                    reads=[ptok, "gT"], writes=["hT_%d" % (t // 4)])

            p0_a(0)
            p0_a(1)
            for t in range(32):
                if t + 2 < 32:
                    p0_a(t + 2)
                p0_b(t)

            wbuf = [sb1("wbuf%d" % i, [128, 8 * 512], BF) for i in range(2)]
            QT = sb1("QT", [128, NT], BF)
            KT = sb1("KT", [128, NT], BF)
            VT = sb1("VT", [128, 32 * 128], BF)
            gate = sb1("gate", [128, NT], BF)
            bA = sb1("bA", [128, 8 * 384], BF)
            bB = sb1("bB", [128, 24 * 256], BF)
            rawq = [sb1("rawq%d" % i, [128, 512], BF) for i in range(2)]
            rawk = [sb1("rawk%d" % i, [128, 512], BF) for i in range(2)]
            sqq = [sb1("sqq%d" % i, [128, 512], BF) for i in range(2)]
            sqk = [sb1("sqk%d" % i, [128, 512], BF) for i in range(2)]
            lnq = [sb1("lnq%d" % i, [128, 512], F32) for i in range(2)]
            lnk = [sb1("lnk%d" % i, [128, 512], F32) for i in range(2)]
            rq, rk = lnq, lnk
            PT = [sb1("PT%d" % i, [128, 2 * 384], BF) for i in range(5)]
            fln = [sb1("fln%d" % i, [128, 512], F32) for i in range(2)]
            frec = [sb1("frec%d" % i, [128, 512], F32) for i in range(2)]
            ft = [sb1("ft%d" % i, [128, 512], F32) for i in range(2)]
            yblk = [sb1("yblk%d" % i, [128, 512], BF) for i in range(2)]

            P.op("pool", lambda e: e.dma_start(out=AP(bA, 0, [[8 * 384, 128], [384, 8], [1, 384]]),
                                               in_=AP(bA_d, 0, [[384, 128], [128 * 384, 8], [1, 384]])),
                 writes=["bA"], dma=True, stream="c_bA")
            for g in range(3):
                P.op("pool", lambda e, g=g: e.dma_start(
                    out=AP(bB, g * 8 * 256, [[24 * 256, 128], [256, 8], [1, 256]]),
                    in_=AP(bB_d, g * 8 * 128 * 256, [[256, 128], [128 * 256, 8], [1, 256]])),
                    writes=["bB%d" % g], dma=True, stream="c_bB%d" % g)
                P.op("act", lambda e, g=g: e.activation(out=bB[:, g * 2048:(g + 1) * 2048],
                                                        in_=bB[:, g * 2048:(g + 1) * 2048], func=AF.Exp),
                     reads=["bB%d" % g], writes=["bB%d" % g])

            def load_w(ui):
                wp = ui % 2
                n = 512 if UNITS[ui]["gate"] else 384
                P.op("pool", lambda e: e.dma_start(
                    out=AP(wbuf[wp], 0, [[8 * 512, 128], [512, 8], [1, n]]),
                    in_=AP(wcat_d, ui * 512, [[WCOLS, 128], [128 * WCOLS, 8], [1, n]])),
                    writes=["wbuf%d" % wp], dma=True, stream="wbuf%d" % wp)

            rot = [0]

            def next_bank():
                k = rot[0] % 8
                rot[0] += 1
                return bank(k)

            fin_i = [0]
            yscr_last = {}

            def finalize_out(u, blk, num_ap, den_ap, num_tok, den_tok, den_bias):
                i = fin_i[0] % 2
                fin_i[0] += 1
                if den_bias is None:
                    P.op("act", lambda e: e.activation(out=fln[i][:, :], in_=den_ap, func=AF.Ln),
                         reads=[den_tok], writes=["fln%d" % i])
                else:
                    P.op("act", lambda e: e.activation(out=fln[i][:, :], in_=den_ap, func=AF.Ln, bias=den_bias),
                         reads=[den_tok, "esink"], writes=["fln%d" % i])
                P.op("act", lambda e: e.activation(out=frec[i][:, :], in_=fln[i][:, :], func=AF.Exp, scale=-1.0),
                     reads=["fln%d" % i], writes=["frec%d" % i])
                P.op("dve", lambda e: e.tensor_tensor(out=ft[i][:, :], in0=num_ap, in1=frec[i][:, :], op=ALU.mult),
                     reads=[num_tok, "frec%d" % i], writes=["ft%d" % i])
                P.op("pool", lambda e: e.tensor_tensor(out=yblk[i][:, :], in0=ft[i][:, :],
                                                       in1=gate[:, blk * 512:(blk + 1) * 512], op=ALU.mult),
                     reads=["ft%d" % i, "gate_%d" % blk], writes=["yblk%d" % i])
                ch = u["ychunk"]
                o = P.op("sp", lambda e: e.dma_start(out=AP(yscr_d, ch * 128 * NT + blk * 512, [[NT, 128], [1, 512]]),
                                                     in_=yblk[i][:, :]),
                         reads=["yblk%d" % i], dma=True, stream="yblk%d" % i)
                yscr_last[i] = o

            load_w(0)

            def do_unit(ui, u):
                wp = ui % 2
                D, W = u["D"], u["W"]
                L = NT // D
                if ui + 1 < len(UNITS):
                    load_w(ui + 1)
                wtok = "wbuf%d" % wp
                gi = 0 if u["kind"] == "A" else 1

                def tok_ap(tb, c):
                    if D == 16:
                        return AP(hT, c * NT + 2 * tb, [[8 * NT, 128], [1, 2], [16, 256]])
                    cls = (tb * 512) // L
                    m0 = (tb * 512) % L
                    return AP(hT, c * NT + m0 * D + cls, [[8 * NT, 128], [D, 512]])

                def nat_ap(tb, c):
                    return AP(hT, c * NT + tb * 512, [[8 * NT, 128], [1, 512]])

                RB = 512 // D

                def ps_perm(t_, o_):
                    return AP(t_, o_, [[1024, 128], [1, D], [D, RB]])

                def sb_perm(t_):
                    return AP(t_, 0, [[512, 128], [RB, D], [1, RB]])

                def dil_dst(t_, tb):
                    return AP(t_, RB * tb, [[NT, 128], [L, D], [1, RB]])

                reuse_kv = (u["kind"] == "A" and ui % 2 == 1)

                def tile_ap(ti, c):
                    cls = (ti * 128) // L
                    m0 = (ti * 128) % L
                    return AP(hT, c * NT + m0 * D + cls, [[8 * NT, 128], [D, 128]])

                st = {}

                def stage1(tb):
                    i2 = tb % 2
                    (tq, oq, kq), (tk, ok, kk), (tv, ov, kv) = next_bank(), next_bank(), next_bank()

                    def pq(e):
                        ins = None
                        for c in range(8):
                            ins = e.matmul(AP(tq, oq, [[1024, 128], [1, 512]]),
                                           AP(wbuf[wp], c * 512, [[8 * 512, 128], [1, 128]]), nat_ap(tb, c),
                                           start=(c == 0), stop=(c == 7))
                        return ins

                    def pk(e):
                        ins = None
                        for c in range(8):
                            ins = e.matmul(AP(tk, ok, [[1024, 128], [1, 512]]),
                                           AP(wbuf[wp], c * 512 + 128, [[8 * 512, 128], [1, 128]]), nat_ap(tb, c),
                                           start=(c == 0), stop=(c == 7))
                        return ins

                    def pv(e):
                        ins = None
                        for j in range(4):
                            for c in range(8):
                                ins = e.matmul(AP(tv, ov + j * 128, [[1024, 128], [1, 128]]),
                                               tile_ap(tb * 4 + j, c),
                                               AP(wbuf[wp], c * 512 + 256, [[8 * 512, 128], [1, 128]]),
                                               start=(c == 0), stop=(c == 7))
                        return ins
                    P.op("pe", pq, reads=[wtok, "hT_%d" % tb], writes=[kq])
                    if not reuse_kv:
                        P.op("pe", pk, reads=[wtok, "hT_%d" % tb], writes=[kk])
                        P.op("pe", pv, reads=[wtok] + (["hT_%d" % tb] if D == 1 else hT_all), writes=[kv])
                    P.op("act", lambda e: e.activation(out=sb_perm(rawq[i2]), in_=ps_perm(tq, oq),
                                                       func=AF.Copy), reads=[kq], writes=["rawq%d" % i2])
                    P.op("pool", lambda e: e.tensor_tensor(out=sqq[i2][:, :], in0=rawq[i2][:, :], in1=rawq[i2][:, :],
                                                           op=ALU.mult), reads=["rawq%d" % i2], writes=["sqq%d" % i2])
                    if not reuse_kv:
                        P.op("act", lambda e: e.activation(out=sb_perm(rawk[i2]), in_=ps_perm(tk, ok),
                                                           func=AF.Copy), reads=[kk], writes=["rawk%d" % i2])
                        P.op("dve", lambda e: e.tensor_copy(out=VT[:, tb * 512:(tb + 1) * 512],
                                                            in_=AP(tv, ov, [[1024, 128], [1, 512]])),
                             reads=[kv], writes=["VT_%d" % tb])
                        P.op("pool", lambda e: e.tensor_tensor(out=sqk[i2][:, :], in0=rawk[i2][:, :], in1=rawk[i2][:, :],
                                                               op=ALU.mult), reads=["rawk%d" % i2], writes=["sqk%d" % i2])

                def stage2(tb):
                    i2 = tb % 2
                    for (sq_, ln_, r_, raw_, dst, nm) in ((sqq, lnq, rq, rawq, QT, "q"), (sqk, lnk, rk, rawk, KT, "k")):
                        if nm == "k" and reuse_kv:
                            continue
                        tsq, osq, ksq = next_bank()
                        P.op("pe", lambda e, sq_=sq_, tsq=tsq, osq=osq: e.matmul(
                            AP(tsq, osq, [[1024, 128], [1, 512]]), bones[:, :], sq_[i2][:, :], start=True, stop=True),
                            reads=["s%s%d" % (nm, i2) if False else ("sq%s%d" % (nm, i2)), "bones"], writes=[ksq])
                        P.op("act", lambda e, ln_=ln_, tsq=tsq, osq=osq: e.activation(
                            out=ln_[i2][:, :], in_=AP(tsq, osq, [[1024, 128], [1, 512]]), func=AF.Ln, bias=64.0 * EPS),
                            reads=[ksq], writes=["ln%s%d" % (nm, i2)])
                        P.op("act", lambda e, ln_=ln_, r_=r_: e.activation(
                            out=r_[i2][:, :], in_=ln_[i2][:, :], func=AF.Exp, scale=-0.5),
                            reads=["ln%s%d" % (nm, i2)], writes=["ln%s%d" % (nm, i2)])
                        if nm == "q":
                            P.op("dve", lambda e, r_=r_, raw_=raw_, dst=dst: e.tensor_tensor(
                                out=dil_dst(dst, tb), in0=sb_perm(raw_[i2]), in1=sb_perm(r_[i2]), op=ALU.mult),
                                reads=["rawq%d" % i2, "lnq%d" % i2], writes=["QT_%d" % tb])
                        else:
                            P.op("dve", lambda e, r_=r_, raw_=raw_, dst=dst: e.scalar_tensor_tensor(
                                out=dil_dst(dst, tb), in0=sb_perm(raw_[i2]), scalar=g8[:, gi:gi + 1],
                                in1=sb_perm(r_[i2]), op0=ALU.mult, op1=ALU.mult),
                                reads=["rawk%d" % i2, "lnk%d" % i2, "g8_%d" % gi], writes=["KT_%d" % tb])

                stage1(0)
                for tb in range(1, 8):
                    stage1(tb)
                    stage2(tb - 1)
                stage2(7)
                if u["gate"]:
                    for tb in range(8):
                        tg, og, kg = next_bank()

                        def pg(e, tb=tb, tg=tg, og=og):
                            ins = None
                            for c in range(8):
                                ins = e.matmul(AP(tg, og, [[1024, 128], [1, 512]]),
                                               AP(wbuf[wp], c * 512 + 384, [[8 * 512, 128], [1, 128]]),
                                               AP(hT, c * NT + tb * 512, [[8 * NT, 128], [1, 512]]),
                                               start=(c == 0), stop=(c == 7))
                            return ins
                        P.op("pe", pg, reads=[wtok, "hT_%d" % tb], writes=[kg])
                        P.op("act", lambda e, tb=tb, tg=tg, og=og: e.activation(
                            out=gate[:, tb * 512:(tb + 1) * 512], in_=AP(tg, og, [[1024, 128], [1, 512]]), func=AF.Silu),
                            reads=[kg], writes=["gate_%d" % tb])

                QBS = min(512, L)
                NJ = 128 + 2 * W
                if u["kind"] == "A":
                    btile, bh0, nbh, btok = bA, 2 * u["ci"], 8, "bA"
                else:
                    btile, bh0, nbh, btok = bB, u["g"] * 8 + 2 * u["ci"], 24, "bB%d" % u["g"]
                pieces = []
                qbi = 0
                for cls in range(D):
                    for j in range(L // QBS):
                        q0 = j * QBS
                        lst = []
                        for kt in range(L // 128):
                            qa = max(128 * kt - W, q0, 0)
                            qb_ = min(128 * kt + 128 + W, q0 + QBS, L)
                            if qb_ > qa:
                                lst.append((kt, qa, qb_))
                        for n, (kt, qa, qb_) in enumerate(lst):
                            pieces.append(dict(cls=cls, j=j, q0=q0, kt=kt, qa=qa, qb=qb_, first=(n == 0),
                                               last=(n == len(lst) - 1), qbi=qbi))
                        qbi += 1
                allQ = ["QT_%d" % t for t in range(8)]
                allK = ["KT_%d" % t for t in range(8)]
                allV = ["VT_%d" % t for t in range(8)]

                isB = (u["kind"] == "B")
                NSLOT = 2
                DEPTH = 2 if isB else 1
                NPT = 5

                def slot(pi):
                    sl = pi % NSLOT
                    return ps[sl], 0, ["ps%da" % sl, "ps%db" % sl]

                def emit_S(pi, pc):
                    t_, co, toks = slot(pi)
                    nq = pc["qb"] - pc["qa"]
                    nk = pc["cls"] * L + pc["kt"] * 128
                    nqo = pc["cls"] * L + pc["qa"]
                    j0 = pc["qa"] - (128 * pc["kt"] - W)

                    def f(e):
                        ins = None
                        for h in range(2):
                            ins = e.matmul(AP(t_, h * 512 + co, [[1024, 128], [1, nq]]),
                                           AP(KT, h * 64 * NT + nk, [[NT, 64], [1, 128]]),
                                           AP(QT, h * 64 * NT + nqo, [[NT, 64], [1, nq]]), start=True, stop=isB)
                        if not isB:
                            for h in range(2):
                                ins = e.matmul(AP(t_, h * 512 + co, [[1024, 128], [1, nq]]), ident[:, :],
                                               AP(btile, (bh0 + h) * NJ + j0, [[nbh * NJ, 128], [1, nq]]),
                                               start=False, stop=True)
                        return ins
                    P.op("pe", f, reads=allQ + allK + ([] if isB else [btok]) + ["ident"], writes=toks)

                def emit_E(pi, pc):
                    t_, co, toks = slot(pi)
                    nq = pc["qb"] - pc["qa"]
                    p3 = pi % NPT
                    j0 = pc["qa"] - (128 * pc["kt"] - W)
                    P.op("act", lambda e: e.activation(out=AP(PT[p3], 0, [[768, 128], [384, 2], [1, nq]]),
                                                       in_=AP(t_, co, [[1024, 128], [512, 2], [1, nq]]), func=AF.Exp),
                         reads=toks, writes=["PT%d" % p3])
                    if isB:
                        P.op("dve", lambda e: e.tensor_tensor(
                            out=AP(PT[p3], 0, [[768, 128], [384, 2], [1, nq]]),
                            in0=AP(PT[p3], 0, [[768, 128], [384, 2], [1, nq]]),
                            in1=AP(btile, bh0 * NJ + j0, [[nbh * NJ, 128], [NJ, 2], [1, nq]]), op=ALU.mult),
                            reads=["PT%d" % p3, btok], writes=["PT%d" % p3])

                def emit_PV(pi, pc):
                    nq = pc["qb"] - pc["qa"]
                    p3 = pi % NPT
                    acc = 2 + pc["qbi"] % 2
                    qc = pc["qa"] - pc["q0"]
                    tile = (pc["cls"] * L) // 128 + pc["kt"]

                    def f(e):
                        ins = None
                        for h in range(2):
                            ins = e.matmul(AP(ps[acc], h * 64 * 1024 + qc, [[1024, 64], [1, nq]]),
                                           AP(VT, tile * 128 + h * 64, [[32 * 128, 128], [1, 64]]),
                                           AP(PT[p3], h * 384, [[768, 128], [1, nq]]),
                                           start=pc["first"], stop=pc["last"], skip_group_check=True,
                                           tile_position=(0, h * 64))
                        for h in range(2):
                            ins = e.matmul(AP(ps[acc], h * 64 * 1024 + 512 + qc, [[1024, 64], [1, nq]]),
                                           ones64[:, :],
                                           AP(PT[p3], h * 384, [[768, 128], [1, nq]]),
                                           start=pc["first"], stop=pc["last"], skip_group_check=True,
                                           tile_position=(0, h * 64))
                        return ins
                    P.op("pe", f, reads=["PT%d" % p3, "ones64"] + allV, writes=["ps%da" % acc, "ps%db" % acc])

                def emit_fin(pc):
                    acc = 2 + pc["qbi"] % 2
                    ntok, dtok = "ps%da" % acc, "ps%db" % acc
                    num_ap = AP(ps[acc], 0, [[1024, 128], [1, QBS]])
                    den_ap = AP(ps[acc], 512, [[1024, 128], [1, QBS]])
                    if u["kind"] == "A":
                        finalize_out(u, pc["j"], num_ap, den_ap, ntok, dtok, esink[:, u["ci"]:u["ci"] + 1])
                        return
                    if D == 1:
                        off, dims = pc["q0"], [[1, QBS]]
                    else:
                        off, dims = pc["q0"] * D + pc["cls"], [[D, QBS]]
                    vN = AP(accN, off, [[NT, 128]] + dims)
                    vD = AP(accD, off, [[NT, 128]] + dims)
                    if u["g"] == 0:
                        P.op("dve", lambda e: e.tensor_copy(out=vN, in_=num_ap), reads=[ntok], writes=["accN"])
                        P.op("act", lambda e: e.activation(out=vD, in_=den_ap, func=AF.Copy), reads=[dtok], writes=["accD"])
                    else:
                        P.op("dve", lambda e: e.tensor_tensor(out=vN, in0=vN, in1=num_ap, op=ALU.add),
                             reads=[ntok, "accN"], writes=["accN"])
                        P.op("dve", lambda e: e.tensor_tensor(out=vD, in0=vD, in1=den_ap, op=ALU.add),
                             reads=[dtok, "accD"], writes=["accD"])

                for pi in range(min(DEPTH, len(pieces))):
                    emit_S(pi, pieces[pi])
                pending = []
                for pi, pc in enumerate(pieces):
                    emit_E(pi, pc)
                    if pi + DEPTH < len(pieces):
                        emit_S(pi + DEPTH, pieces[pi + DEPTH])
                    emit_PV(pi, pc)
                    while pending and pending[0][0] <= pi:
                        emit_fin(pending.pop(0)[1])
                    if pc["last"]:
                        pending.append((pi + 1, pc))
                while pending:
                    emit_fin(pending.pop(0)[1])
                if u["kind"] == "B" and u["g"] == 2:
                    for blk in range(8):
                        finalize_out(u, blk, accN[:, blk * 512:(blk + 1) * 512], accD[:, blk * 512:(blk + 1) * 512],
                                     "accN", "accD", None)

            for ui, u in enumerate(UNITS):
                do_unit(ui, u)
            o_end = P.op("sp", None)
            for o in yscr_last.values():
                o_end.deps.append(o)
            if debug:
                dd = []
                dd.append(P.op("sp", lambda e: e.dma_start(out=hTd.ap(), in_=hT[:, :]), reads=hT_all, dma=True, stream="dbg0"))
                dd.append(P.op("sp", lambda e: e.dma_start(out=accNd.ap(), in_=accN[:, :]), reads=["accN"], dma=True, stream="dbg1"))
                dd.append(P.op("sp", lambda e: e.dma_start(out=accDd.ap(), in_=accD[:, :]), reads=["accD"], dma=True, stream="dbg2"))
                dd.append(P.op("sp", lambda e: e.dma_start(out=QTd.ap(), in_=QT[:, :]), reads=["QT_%d" % t for t in range(8)], dma=True, stream="dbg3"))
                dd.append(P.op("sp", lambda e: e.dma_start(out=KTd.ap(), in_=KT[:, :]), reads=["KT_%d" % t for t in range(8)], dma=True, stream="dbg4"))
                dd.append(P.op("sp", lambda e: e.dma_start(out=VTd.ap(), in_=VT[:, :]), reads=["VT_%d" % t for t in range(8)], dma=True, stream="dbg5"))
                dd.append(P.op("sp", lambda e: e.dma_start(out=gated.ap(), in_=gate[:, :]), reads=["gate_%d" % t for t in range(8)], dma=True, stream="dbg6"))
                o_e2 = P.op("sp", None)
                o_e2.deps.extend(dd)
            with nc.Block() as block:
                P.emit(block)

        with ExitStack() as es2:
            def sb2(name, shape, dt):
                return es2.enter_context(nc.sbuf_tensor(name, shape, dt))
            Wmg = sb2("Wmg", [128, 8 * 2048], BF)
            Wa = sb2("Wa", [128, 4 * 1024], BF)
            Wb = sb2("Wb", [128, 4 * 1024], BF)
            Wo = sb2("Wo", [128, 8 * 1024], BF)
            ybuf = [sb2("ybuf%d" % i, [128, 8 * 512], BF) for i in range(2)]
            g0 = [sb2("g0_%d" % i, [128, 512], BF) for i in range(2)]
            g1 = [sb2("g1_%d" % i, [128, 512], BF) for i in range(2)]
            t0 = [sb2("t0_%d" % i, [128, 512], F32) for i in range(2)]
            t1 = [sb2("t1_%d" % i, [128, 512], F32) for i in range(2)]
            mT = [sb2("mT%d" % i, [128, 8 * 512], BF) for i in range(2)]
            NXR = 3
            xres = [sb2("xres%d" % i, [128, 512], F32) for i in range(NXR)]
            yo = [sb2("yo%d" % i, [128, 512], F32) for i in range(NXR)]

            P.op("pool", lambda e: e.dma_start(out=AP(Wa, 0, [[4 * 1024, 128], [1024, 4], [1, 1024]]),
                                               in_=AP(wa_d, 0, [[DM, 128], [128 * DM, 4], [1, 1024]])),
                 writes=["Wa"], dma=True, stream="Wa")
            P.op("pool", lambda e: e.dma_start(out=AP(Wb, 0, [[4 * 1024, 128], [1024, 4], [1, 1024]]),
                                               in_=AP(wb_d, 0, [[DM, 128], [128 * DM, 4], [1, 1024]])),
                 writes=["Wb"], dma=True, stream="Wb")
            for qd in range(4):
                for half in range(2):
                    P.op("pool", lambda e, half=half, qd=qd: e.dma_start(
                        out=AP(Wmg, half * 1024 + qd * 256, [[8 * 2048, 128], [2048, 8], [1, 256]]),
                        in_=AP(wcat_d, MG_OFF + half * 1024 + qd * 256, [[WCOLS, 128], [128 * WCOLS, 8], [1, 256]])),
                        writes=["Wmg%d_%d" % (half, qd)], dma=True, stream=("Wa", "Wb")[half])
            P.op("pool", lambda e: e.dma_start(out=AP(Wo, 0, [[8 * 1024, 128], [1024, 8], [1, 1024]]),
                                               in_=AP(wo_d, 0, [[DM, 128], [128 * DM, 8], [1, 1024]])),
                 writes=["Wo"], dma=True, stream="Wa")
            rot2 = [0]

            def nb2():
                k = rot2[0] % 8
                rot2[0] += 1
                return bank(k)
            out_dmas = []
            oi = [0]

            def emit_xload(k):
                tb_, rem = divmod(k, 8)
                tt, nh = divmod(rem, 2)
                i3 = k % NXR
                r0 = tb_ * 512 + tt * 128
                P.op("sp", lambda e: e.dma_start(out=xres[i3][:, :], in_=x_d.ap()[r0:r0 + 128, nh * 512:(nh + 1) * 512]),
                     writes=["xres%d" % i3], dma=True, stream="xres%d" % i3)

            def emit_out(tb):
                mp = tb % 2
                for tt in range(4):
                    for nh in range(2):
                        k = tb * 8 + tt * 2 + nh
                        i3 = k % NXR
                        if k == 0:
                            for kk in range(NXR - 1):
                                emit_xload(kk)
                        to, oo, ko = nb2()
                        r0 = tb * 512 + tt * 128

                        def po(e, tt=tt, nh=nh, to=to, oo=oo):
                            ins = None
                            for c in range(8):
                                ins = e.matmul(AP(to, oo, [[1024, 128], [1, 512]]),
                                               AP(mT[mp], c * 512 + tt * 128, [[8 * 512, 128], [1, 128]]),
                                               AP(Wo, c * 1024 + nh * 512, [[8 * 1024, 128], [1, 512]]),
                                               start=(c == 0), stop=(c == 7))
                            return ins
                        P.op("pe", po, reads=["mT%d_%d" % (mp, c) for c in range(8)] + ["Wo"], writes=[ko])
                        P.op("dve", lambda e, i3=i3, to=to, oo=oo: e.tensor_tensor(
                            out=yo[i3][:, :], in0=AP(to, oo, [[1024, 128], [1, 512]]), in1=xres[i3][:, :], op=ALU.add),
                            reads=[ko, "xres%d" % i3], writes=["yo%d" % i3])
                        if k + NXR - 1 < 64:
                            emit_xload(k + NXR - 1)
                        o = P.op("sp", lambda e, i3=i3, r0=r0, nh=nh: e.dma_start(
                            out=y_d.ap()[r0:r0 + 128, nh * 512:(nh + 1) * 512], in_=yo[i3][:, :]),
                            reads=["yo%d" % i3], dma=True, stream="yo%d" % i3)
                        out_dmas.append(o)

            gi_ = [0]
            for tb in range(8):
                yp = tb % 2
                mp = tb % 2
                P.op("pool", lambda e, tb=tb, yp=yp: e.dma_start(
                    out=AP(ybuf[yp], 0, [[8 * 512, 128], [512, 8], [1, 512]]),
                    in_=AP(yscr_d, tb * 512, [[NT, 128], [128 * NT, 8], [1, 512]])),
                    writes=["ybuf%d" % yp], dma=True, stream="ybuf%d" % yp)
                for c in range(8):
                    i2 = gi_[0] % 2
                    gi_[0] += 1
                    (tg0, og0, kg0), (tg1, og1, kg1), (ta, oa, ka), (tbb, obb, kbb) = nb2(), nb2(), nb2(), nb2()

                    def pgate(e, br, tg, og, c=c, tb=tb):
                        ins = None
                        for k in range(8):
                            ins = e.matmul(AP(tg, og, [[1024, 128], [1, 512]]),
                                           AP(Wmg, k * 2048 + br * 1024 + c * 128, [[8 * 2048, 128], [1, 128]]),
                                           AP(hT, k * NT + tb * 512, [[8 * NT, 128], [1, 512]]),
                                           start=(k == 0), stop=(k == 7))
                        return ins

                    def pbr(e, Wx, yoff, tx, ox, c=c, yp=yp):
                        ins = None
                        for k in range(4):
                            ins = e.matmul(AP(tx, ox, [[1024, 128], [1, 512]]),
                                           AP(Wx, k * 1024 + c * 128, [[4 * 1024, 128], [1, 128]]),
                                           AP(ybuf[yp], (yoff + k) * 512, [[8 * 512, 128], [1, 512]]),
                                           start=(k == 0), stop=(k == 3))
                        return ins
                    P.op("pe", lambda e, tg0=tg0, og0=og0, f=pgate: f(e, 0, tg0, og0), reads=["Wmg0_%d" % (c // 2), "hT_%d" % tb], writes=[kg0])
                    P.op("pe", lambda e, tg1=tg1, og1=og1, f=pgate: f(e, 1, tg1, og1), reads=["Wmg1_%d" % (c // 2), "hT_%d" % tb], writes=[kg1])
                    P.op("pe", lambda e, ta=ta, oa=oa, f=pbr: f(e, Wa, 0, ta, oa), reads=["Wa", "ybuf%d" % yp], writes=[ka])
                    P.op("pe", lambda e, tbb=tbb, obb=obb, f=pbr: f(e, Wb, 4, tbb, obb), reads=["Wb", "ybuf%d" % yp], writes=[kbb])
                    P.op("act", lambda e, tg0=tg0, og0=og0, c=c, i2=i2: e.activation(
                        out=g0[i2][:, :], in_=AP(tg0, og0, [[1024, 128], [1, 512]]), func=AF.Sigmoid, bias=bmT[:, c:c + 1]),
                        reads=[kg0, "bmT"], writes=["g0_%d" % i2])
                    P.op("act", lambda e, tg1=tg1, og1=og1, c=c, i2=i2: e.activation(
                        out=g1[i2][:, :], in_=AP(tg1, og1, [[1024, 128], [1, 512]]), func=AF.Sigmoid,
                        bias=bmT[:, 8 + c:9 + c]),
                        reads=[kg1, "bmT"], writes=["g1_%d" % i2])
                    P.op("dve", lambda e, ta=ta, oa=oa, i2=i2: e.tensor_tensor(
                        out=t0[i2][:, :], in0=AP(ta, oa, [[1024, 128], [1, 512]]), in1=g0[i2][:, :], op=ALU.mult),
                        reads=[ka, "g0_%d" % i2], writes=["t0_%d" % i2])
                    P.op("dve", lambda e, tbb=tbb, obb=obb, i2=i2: e.tensor_tensor(
                        out=t1[i2][:, :], in0=AP(tbb, obb, [[1024, 128], [1, 512]]), in1=g1[i2][:, :], op=ALU.mult),
                        reads=[kbb, "g1_%d" % i2], writes=["t1_%d" % i2])
                    P.op("pool", lambda e, c=c, i2=i2, mp=mp: e.tensor_tensor(
                        out=mT[mp][:, c * 512:(c + 1) * 512], in0=t0[i2][:, :], in1=t1[i2][:, :], op=ALU.add),
                        reads=["t0_%d" % i2, "t1_%d" % i2], writes=["mT%d_%d" % (mp, c)])
                if tb >= 1:
                    emit_out(tb - 1)
            emit_out(7)
            o_end = P.op("sp", None)
            o_end.deps.extend(out_dmas[-6:])
            with nc.Block() as block:
                P.emit(block)
    return nc


_NC_CACHE = {}


def _prep_shared(inputs):
    w_in = np.asarray(inputs["w_in"], np.float32)[0]
    cols = []
    for u in UNITS:
        cols += _unit_cols(u)
    cols += list(range(6400, 8448))
    wcat = np.ascontiguousarray(w_in[:, np.asarray(cols)])
    gain = np.asarray(inputs["norm_gain"], np.float32)[0]
    gT = np.ascontiguousarray(gain.reshape(8, 128).T)
    nrm = np.stack([np.tile(np.asarray(inputs[k], np.float32)[0], 2)
                    for k in ("q_norm_a", "k_norm_a", "q_norm_b", "k_norm_b")], axis=1)
    sink = np.asarray(inputs["sink_a"], np.float32)[0]
    sinkT = np.stack([np.repeat(sink[2 * ci:2 * ci + 2], 64) for ci in range(4)], axis=1)
    bm = np.asarray(inputs["b_merge"], np.float32)[0]
    bmT = np.ascontiguousarray(bm.reshape(2, 8, 128).transpose(2, 0, 1).reshape(128, 16))
    rb = np.asarray(inputs["rel_bias"], np.float32)
    p = np.arange(128)[:, None]
    j = np.arange(384)[None, :]
    rel = p - j + 128
    idx = _t5_bucket(rel)
    valid = np.abs(rel) <= 128
    biasA = np.stack([np.where(valid, rb[idx, h], np.float32(NEG)) for h in range(8)]).astype(np.float32)
    j = np.arange(256)[None, :]
    rel = p - j + 64
    valid = np.abs(rel) <= 64
    bl = []
    for g, D in enumerate((1, 4, 16)):
        idx = _t5_bucket(rel * D)
        for h in range(8):
            bl.append(np.where(valid, rb[idx, 8 + 8 * g + h], np.float32(NEG)))
    biasB = np.stack(bl).astype(np.float32)
    ident = np.eye(128, dtype=np.float32)
    bones = np.kron(np.eye(2, dtype=np.float32), np.ones((64, 64), np.float32))
    return dict(wcat=wcat, w_a=np.ascontiguousarray(np.asarray(inputs["w_branch_a"], np.float32)[0]),
                w_b=np.ascontiguousarray(np.asarray(inputs["w_branch_b"], np.float32)[0]),
                w_o=np.ascontiguousarray(np.asarray(inputs["w_out"], np.float32)[0]),
                gT=gT, nrm=np.ascontiguousarray(nrm), sinkT=np.ascontiguousarray(sinkT), bmT=bmT,
                biasA=np.ascontiguousarray(biasA), biasB=np.ascontiguousarray(biasB), ident=ident, bones=bones)


def kernel(**inputs):
    x = np.asarray(inputs["x"], np.float32)
    shared = _prep_shared(inputs)
    if "nc" not in _NC_CACHE:
        _NC_CACHE["nc"] = build_nc()
    nc = _NC_CACHE["nc"]
    in_maps = []
    for b in range(8):
        m = dict(shared)
        m["x"] = np.ascontiguousarray(x[b])
        in_maps.append(m)
    res = run_bass_kernel_spmd(nc, in_maps, core_ids=list(range(8)))
    return np.stack([np.asarray(r["y"], np.float32) for r in res.results], axis=0)
```
